# Optimizing a Trainium2 kernel written in Bass

```python
import jax, jax.numpy as jnp
from jax import lax
import numpy as np

D_MODEL = 1024
BATCH = 16
SEQ = 256
DEPTH = 1
DEC_BATCH = 4
DEC_SEQ = 4096
PAST_LEN = 512

GRID_W = 64
N_HEADS = 8
N_KV = 2
GROUP = N_HEADS // N_KV
HEAD_DIM = 128
R_HEADS = 8
R_DK = 64
R_DV = 128
D_FF = 2816
CONV_W = 3
CHUNK = 128
Q_BLOCK = 128
ROPE_THETA = 10000.0
EPS = 1e-6

ATT_Q = N_HEADS * HEAD_DIM
ATT_KV = N_KV * HEAD_DIM
RET_QK = R_HEADS * R_DK
RET_V = R_HEADS * R_DV
IN_SPLITS = (ATT_Q, ATT_KV, ATT_KV, RET_QK, RET_QK, RET_V, RET_V, D_MODEL, D_MODEL)
D_IN = ATT_Q + 2 * ATT_KV + 2 * RET_QK + 2 * RET_V + 2 * D_MODEL

kernel_name = "hybrid_retention_gqa_dit_step"


def rms_scale(x):
    xf = x.astype(jnp.float32)
    return xf * lax.rsqrt(jnp.mean(xf * xf, axis=-1, keepdims=True) + EPS)


def rmsnorm(x, g):
    return (rms_scale(x) * g.astype(jnp.float32)).astype(x.dtype)


def rope_1d(x, pos):
    half = x.shape[-1] // 2
    inv = ROPE_THETA ** (-jnp.arange(half, dtype=jnp.float32) / half)
    ang = pos.astype(jnp.float32)[:, None] * inv[None, :]
    cos, sin = jnp.cos(ang), jnp.sin(ang)
    x1 = x[..., :half].astype(jnp.float32)
    x2 = x[..., half:].astype(jnp.float32)
    return jnp.concatenate([x1 * cos - x2 * sin, x2 * cos + x1 * sin], axis=-1).astype(x.dtype)


def rope_2d(x):
    L = x.shape[-2]
    rows = L // GRID_W
    t = jnp.arange(rows * GRID_W)
    half = x.shape[-1] // 2
    return jnp.concatenate([rope_1d(x[..., :half], t // GRID_W), rope_1d(x[..., half:], t % GRID_W)], axis=-1)


def adaln(cond, w_mod, b_mod):
    m = jax.nn.silu(cond) @ w_mod + b_mod
    return jnp.split(m[..., None, :], 6, axis=-1)


def attend_blocks(q, k, v):
    B, KV, G, Lq, hd = q.shape
    nb = Lq // Q_BLOCK
    qb = q.reshape(B, KV, G, nb, Q_BLOCK, hd).transpose(3, 0, 1, 2, 4, 5)
    scale = hd ** -0.5

    def one_block(qblk):
        s = jnp.einsum("bkgqd,bksd->bkgqs", qblk, k, preferred_element_type=jnp.float32) * scale
        p = jax.nn.softmax(s, axis=-1)
        return jnp.einsum("bkgqs,bksd->bkgqd", p.astype(v.dtype), v)

    o = lax.map(one_block, qb)
    o = o.transpose(1, 2, 3, 0, 4, 5).reshape(B, KV * G, Lq, hd)
    return o.transpose(0, 2, 1, 3).reshape(B, Lq, KV * G * hd)


def retention_scan(q, k, v, log_gamma, s0):
    B, H, L, dk = q.shape
    dv = v.shape[-1]
    n = L // CHUNK
    qc = q.reshape(B, H, n, CHUNK, dk)
    kc = k.reshape(B, H, n, CHUNK, dk)
    vc = v.reshape(B, H, n, CHUNK, dv)
    i = jnp.arange(CHUNK, dtype=jnp.float32)
    lg = log_gamma[:, None]
    diff = i[:, None] - i[None, :]
    intra_decay = jnp.where(diff >= 0, jnp.exp(lg[:, :, None] * jnp.maximum(diff, 0.0)), 0.0)
    q_decay = jnp.exp(lg * (i + 1.0))
    k_decay = jnp.exp(lg * (CHUNK - 1.0 - i))
    chunk_decay = jnp.exp(log_gamma * CHUNK)[None, :, None, None]
    scores = jnp.einsum("bhnid,bhnjd->bhnij", qc, kc, preferred_element_type=jnp.float32)
    scores = scores * intra_decay[None, :, None]
    intra = jnp.einsum("bhnij,bhnjv->bhniv", scores, vc.astype(jnp.float32))
    kv = jnp.einsum("bhnjd,bhnjv->nbhdv", kc.astype(jnp.float32) * k_decay[None, :, None, :, None],
                    vc.astype(jnp.float32))

    def step(s, kv_n):
        return chunk_decay * s + kv_n, s

    s_final, s_prev = lax.scan(step, s0.astype(jnp.float32), kv)
    cross = jnp.einsum("bhnid,nbhdv->bhniv", qc.astype(jnp.float32) * q_decay[None, :, None, :, None], s_prev)
    return (intra + cross).reshape(B, H, L, dv), s_final


def retention_branch(rq, rk, rv, rg, pos, dec_f, dec_b, s0_f, s0_b):
    B, L, _ = rq.shape
    q = rope_1d(rq.reshape(B, L, R_HEADS, R_DK).transpose(0, 2, 1, 3), pos)
    k = rope_1d(rk.reshape(B, L, R_HEADS, R_DK).transpose(0, 2, 1, 3), pos) * (R_DK ** -0.5)
    v = rv.reshape(B, L, R_HEADS, R_DV).transpose(0, 2, 1, 3)
    lg_f = jax.nn.log_sigmoid(dec_f.astype(jnp.float32))
    lg_b = jax.nn.log_sigmoid(dec_b.astype(jnp.float32))
    o_f, s_f = retention_scan(q, k, v, lg_f, s0_f)
    o_b, s_b = retention_scan(q[:, :, ::-1], k[:, :, ::-1], v[:, :, ::-1], lg_b, s0_b)
    o = rms_scale(o_f + o_b[:, :, ::-1])
    o = o.transpose(0, 2, 1, 3).reshape(B, L, RET_V)
    return jax.nn.silu(rg.astype(jnp.float32)) * o, s_f, s_b


def token_mixer(h, w_in, q_g, k_g, dec_f, dec_b, w_att_o, w_ret_o, w_out, ret_pos, latent_ctx):
    B, L, _ = h.shape
    z = h @ w_in
    q, k, v, rq, rk, rv, rg, ga, gr = jnp.split(z, np.cumsum(IN_SPLITS)[:-1].tolist(), axis=-1)
    q = rmsnorm(q.reshape(B, L, N_KV, GROUP, HEAD_DIM), q_g).transpose(0, 2, 3, 1, 4)
    k = rmsnorm(k.reshape(B, L, N_KV, HEAD_DIM), k_g).transpose(0, 2, 1, 3)
    v = v.reshape(B, L, N_KV, HEAD_DIM).transpose(0, 2, 1, 3)
    if latent_ctx is None:
        k_all, v_all = k, v
        s0_f = jnp.zeros((B, R_HEADS, R_DK, R_DV), jnp.float32)
        s0_b = s0_f
    else:
        ck, cv, s0_f, s0_b = latent_ctx
        q = rope_2d(q)
        k = rope_2d(k)
        k_all = jnp.concatenate([k, ck.astype(k.dtype)], axis=2)
        v_all = jnp.concatenate([v, cv.astype(v.dtype)], axis=2)
    ya = attend_blocks(q, k_all, v_all)
    yr, s_f, s_b = retention_branch(rq, rk, rv, rg, ret_pos, dec_f, dec_b, s0_f, s0_b)
    y = jax.nn.sigmoid(ga) * (ya @ w_att_o) + jax.nn.sigmoid(gr) * (yr.astype(h.dtype) @ w_ret_o)
    return y @ w_out, (k, v, s_f, s_b)


def conv_ffn(h, w_up, conv_w, conv_b, w_down):
    L = h.shape[1]
    u = h @ w_up
    up = jnp.pad(u, ((0, 0), (1, 1), (0, 0)))
    uc = conv_w[0] * up[:, :L] + conv_w[1] * up[:, 1:L + 1] + conv_w[2] * up[:, 2:L + 2] + conv_b
    a, g = jnp.split(uc, 2, axis=-1)
    return (jax.nn.silu(g) * a) @ w_down


def setup_inputs(seed: int = 0) -> dict:
    key = jax.random.key(seed)
    ks = jax.random.split(key, 26)

    def nrm(k, shape, s):
        return jax.random.normal(k, shape, jnp.float32) * s

    base_decay = jnp.log(2.0 ** (5.0 + jnp.arange(R_HEADS, dtype=jnp.float32)) - 1.0)
    return {
        "x_prompt": nrm(ks[0], (BATCH, SEQ, D_MODEL), 1.0),
        "x_sample": nrm(ks[1], (DEC_BATCH, DEC_SEQ, D_MODEL), 1.0),
        "c": nrm(ks[2], (DEC_BATCH, D_MODEL), 1.0),
        "cache_k": nrm(ks[3], (DEC_BATCH, DEPTH, N_KV, PAST_LEN, HEAD_DIM), 1.0),
        "cache_v": nrm(ks[4], (DEC_BATCH, DEPTH, N_KV, PAST_LEN, HEAD_DIM), 1.0),
        "state_ret_fwd": nrm(ks[5], (DEC_BATCH, DEPTH, R_HEADS, R_DK, R_DV), 1.0),
        "state_ret_bwd": nrm(ks[6], (DEC_BATCH, DEPTH, R_HEADS, R_DK, R_DV), 1.0),
        "c_ctx": nrm(ks[7], (D_MODEL,), 1.0),
        "norm1_g": 1.0 + nrm(ks[8], (DEPTH, D_MODEL), 0.05),
        "norm2_g": 1.0 + nrm(ks[9], (DEPTH, D_MODEL), 0.05),
        "w_mod": nrm(ks[10], (DEPTH, D_MODEL, 6 * D_MODEL), D_MODEL ** -0.5),
        "b_mod": nrm(ks[11], (DEPTH, 6 * D_MODEL), 0.02),
        "w_in": nrm(ks[12], (DEPTH, D_MODEL, D_IN), D_MODEL ** -0.5),
        "q_norm_g": 1.0 + nrm(ks[13], (DEPTH, HEAD_DIM), 0.05),
        "k_norm_g": 1.0 + nrm(ks[14], (DEPTH, HEAD_DIM), 0.05),
        "decay_fwd": base_decay + nrm(ks[15], (DEPTH, R_HEADS), 0.1),
        "decay_bwd": base_decay + nrm(ks[16], (DEPTH, R_HEADS), 0.1),
        "w_att_o": nrm(ks[17], (DEPTH, ATT_Q, D_MODEL), ATT_Q ** -0.5),
        "w_ret_o": nrm(ks[18], (DEPTH, RET_V, D_MODEL), RET_V ** -0.5),
        "w_out": nrm(ks[19], (DEPTH, D_MODEL, D_MODEL), D_MODEL ** -0.5),
        "w_up": nrm(ks[20], (DEPTH, D_MODEL, 2 * D_FF), D_MODEL ** -0.5),
        "conv_w": nrm(ks[21], (DEPTH, CONV_W, 2 * D_FF), CONV_W ** -0.5),
        "conv_b": nrm(ks[22], (DEPTH, 2 * D_FF), 0.02),
        "w_down": nrm(ks[23], (DEPTH, D_FF, D_MODEL), D_FF ** -0.5),
        "final_g": 1.0 + nrm(ks[24], (D_MODEL,), 0.05),
    }


def reference(x_prompt, x_sample, c, cache_k, cache_v, state_ret_fwd, state_ret_bwd, c_ctx,
              norm1_g, norm2_g, w_mod, b_mod, w_in, q_norm_g, k_norm_g, decay_fwd, decay_bwd,
              w_att_o, w_ret_o, w_out, w_up, conv_w, conv_b, w_down, final_g):
    ctx_p = x_prompt.shape[1]
    ctx_s = cache_k.shape[3]
    lat = x_sample.shape[1]
    pos_ctx = jnp.arange(ctx_p)
    pos_lat = ctx_s + jnp.arange(lat)
    xp, xs = x_prompt, x_sample
    nk, nv, nsf, nsb = [], [], [], []
    for l in range(DEPTH):
        mix_w = (w_in[l], q_norm_g[l], k_norm_g[l], decay_fwd[l], decay_bwd[l], w_att_o[l], w_ret_o[l], w_out[l])
        sh1, sc1, g1, sh2, sc2, g2 = adaln(c_ctx, w_mod[l], b_mod[l])
        h = rmsnorm(xp, norm1_g[l]) * (1.0 + sc1) + sh1
        y, (k_c, v_c, sf_c, sb_c) = token_mixer(h, *mix_w, pos_ctx, None)
        xp = xp + g1 * y
        h = rmsnorm(xp, norm2_g[l]) * (1.0 + sc2) + sh2
        xp = xp + g2 * conv_ffn(h, w_up[l], conv_w[l], conv_b[l], w_down[l])
        nk.append(k_c)
        nv.append(v_c)
        nsf.append(sf_c)
        nsb.append(sb_c)
        sh1, sc1, g1, sh2, sc2, g2 = adaln(c, w_mod[l], b_mod[l])
        h = rmsnorm(xs, norm1_g[l]) * (1.0 + sc1) + sh1
        ctx = (cache_k[:, l], cache_v[:, l], state_ret_fwd[:, l], state_ret_bwd[:, l])
        y, _ = token_mixer(h, *mix_w, pos_lat, ctx)
        xs = xs + g1 * y
        h = rmsnorm(xs, norm2_g[l]) * (1.0 + sc2) + sh2
        xs = xs + g2 * conv_ffn(h, w_up[l], conv_w[l], conv_b[l], w_down[l])
    y_prompt = rmsnorm(xp, final_g)
    y_sample = rmsnorm(xs, final_g)
    new_cache_k = jnp.stack(nk, axis=1)
    new_cache_v = jnp.stack(nv, axis=1)
    new_state_ret_fwd = jnp.stack(nsf, axis=1)
    new_state_ret_bwd = jnp.stack(nsb, axis=1)
    return (y_prompt, y_sample, new_cache_k, new_cache_v, new_state_ret_fwd, new_state_ret_bwd)
```

```python
import contextlib
import numpy as np
import concourse.bass as bass
import concourse.mybir as mybir
from concourse.bass_utils import run_bass_kernel_spmd

F32 = mybir.dt.float32
BF16 = mybir.dt.bfloat16
AF = mybir.ActivationFunctionType
ALU = mybir.AluOpType
AX = mybir.AxisListType

D = 1024
NT_S = 32
FULL_S = 17
EPS = 1e-6
SCALE = 128.0 ** -0.5
ENG_ATTR = {"pe": "tensor", "act": "scalar", "dve": "vector", "pool": "gpsimd", "sp": "sync"}
NDSEM = 96


class Sched:
    def __init__(self, nc, esems, dsems):
        self.nc = nc
        self.esems = esems
        self.dsems = dsems
        self.ops = []
        self.slot = 0.0
        self.ecnt = {k: 0 for k in esems}
        self.dcnt = [0] * len(dsems)
        self.pool_ids = list(range(0, 24))
        self.sp_ids = list(range(24, len(dsems)))
        self.seen = {e: {} for e in ENG_ATTR}

    def op(self, eng, fn, reads=(), writes=(), dma=False, sk=None):
        if dma and sk is None:
            sk = reads[0] if len(reads) else writes[0]
        self.ops.append((eng, fn, tuple(reads), tuple(writes), dma, sk, self.slot))

    def emit_phase(self):
        ops = [o[:6] for o in sorted(self.ops, key=lambda o: o[6])]
        self.ops = []
        self.slot = 0.0
        n = len(ops)
        last_w, readers = {}, {}
        deps = [None] * n
        has_dep = [False] * n
        for i, (eng, fn, rd, wr, dma, sk) in enumerate(ops):
            d = set()
            for r in rd:
                if r in last_w:
                    d.add(last_w[r])
                if isinstance(r, str) and len(r) > 1 and r[0] == "p" and r[1].isupper():
                    d.update(x for x in readers.get(r, ()) if ops[x][0] != eng)
            for w in wr:
                if w in last_w:
                    d.add(last_w[w])
                d.update(readers.get(w, ()))
            d.discard(i)
            d = {x for x in d if not (eng == "pe" and ops[x][0] == "pe" and not ops[x][4] and not dma)}
            deps[i] = d
            for x in d:
                has_dep[x] = True
            for w in wr:
                last_w[w] = i
                readers[w] = []
            for r in rd:
                if r not in wr:
                    readers.setdefault(r, []).append(i)
        last_of = {}
        for i, (eng, fn, rd, wr, dma, sk) in enumerate(ops):
            if dma:
                has_dep[i] = True
            else:
                last_of[eng] = i
        for i in last_of.values():
            has_dep[i] = True
        semof = [None] * n
        val = [0] * n
        dmap = {}
        for i, (eng, fn, rd, wr, dma, sk) in enumerate(ops):
            if not has_dep[i]:
                continue
            if dma:
                if (eng, sk) not in dmap:
                    pool_ids = self.pool_ids if eng == "pool" else self.sp_ids
                    used = sum(1 for (e2, _k) in dmap if (e2 == "pool") == (eng == "pool"))
                    assert used < len(pool_ids), "out of dma semaphores"
                    dmap[(eng, sk)] = pool_ids[used]
                j = dmap[(eng, sk)]
                self.dcnt[j] += 16
                semof[i] = ("d", j)
                val[i] = self.dcnt[j]
            else:
                self.ecnt[eng] += 1
                semof[i] = ("e", eng)
                val[i] = self.ecnt[eng]
        final = {("e", k): v for k, v in self.ecnt.items()}
        for j in dmap.values():
            final[("d", j)] = self.dcnt[j]

        def semh(k):
            return self.esems[k[1]] if k[0] == "e" else self.dsems[k[1]]

        def mk(engname):
            def body(e):
                seen = self.seen[engname]
                for i, (eng, fn, rd, wr, dma, sk) in enumerate(ops):
                    if eng != engname:
                        continue
                    need = {}
                    for x in deps[i]:
                        k = semof[x]
                        need[k] = max(need.get(k, 0), val[x])
                    for k, v in need.items():
                        if seen.get(k, 0) < v:
                            e.wait_ge(semh(k), v)
                            seen[k] = v
                    ins = fn(e)
                    if semof[i] is not None:
                        ins.then_inc(semh(semof[i]), 16 if dma else 1)
                for k, v in final.items():
                    if v > 0 and seen.get(k, 0) < v:
                        e.wait_ge(semh(k), v)
                        seen[k] = v
            return body

        with self.nc.Block() as block:
            for engname, attr in ENG_ATTR.items():
                getattr(block, attr)(mk(engname))


class Rot:
    def __init__(self, bufs, name):
        self.bufs = bufs
        self.name = name
        self.i = 0

    def next(self):
        j = self.i % len(self.bufs)
        self.i += 1
        return self.bufs[j], "%s#%d" % (self.name, j)


def build_nc(flags):
    nc = bass.Bass("TRN2", target_bir_lowering=False)

    def din(name, shape, dt=F32):
        return nc.dram_tensor(name, list(shape), dt, kind="ExternalInput").ap()

    def dout(name, shape, dt=F32):
        return nc.dram_tensor(name, list(shape), dt, kind="ExternalOutput").ap()

    def dscr(name, shape, dt):
        return nc.dram_tensor(name, list(shape), dt, kind="Internal").ap()

    xs = din("xs", [4096, D])
    xp = din("xp", [512, D])
    cT = din("cT", [128, 8, 2])
    ck = din("ck", [2, 512, 128])
    cv = din("cv", [2, 512, 128])
    s0f = din("s0f", [64, 8, 128])
    s0b = din("s0b", [64, 8, 128])
    dec = din("dec", [1, 32])
    w_in = din("w_in", [128, 8, 6656])
    w_mod = din("w_mod", [128, 8, 6144])
    b_mod = din("b_mod", [1, 6144])
    w_ao = din("w_ao", [128, 8, 1024])
    w_ro = din("w_ro", [128, 8, 1024])
    w_o = din("w_o", [128, 8, 1024])
    w_up = din("w_up", [128, 8, 5632])
    w_dn = din("w_dn", [128, 22, 1024])
    n1g = din("n1g", [1, D])
    n2g = din("n2g", [1, D])
    fg = din("fg", [1, D])
    qg = din("qg", [1, 128])
    kg = din("kg", [1, 128])
    convs = din("convs", [128, 4, 44])
    convp = din("convp", [128, 4, 44])
    rope_att = din("rope_att", [4096, 2, 128])
    rope_rs = din("rope_rs", [4096, 2, 64])
    rope_rp = din("rope_rp", [256, 2, 64])
    tabs = din("tabs", [128, 4, 128])
    tabe = din("tabe", [128, 4])

    ys = dout("ys", [2048, D])
    yp = dout("yp", [512, D])
    nk = dout("nk", [2, 2, 256, 128])
    nv = dout("nv", [2, 2, 256, 128])
    nsf = dout("nsf", [2, 64, 8, 128])
    nsb = dout("nsb", [2, 64, 8, 128])

    NFT = FULL_S + 4
    hT_scr = dscr("hT_scr", [NFT, 128, 1024], BF16)
    ya_scr = dscr("ya_scr", [NFT, 128, 1024], BF16)
    yr_scr = dscr("yr_scr", [NFT, 128, 1024], BF16)
    h2_scr = dscr("h2_scr", [NFT, 128, 1024], BF16)
    xs_scr = dscr("xs_scr", [NFT, 128, 1024], F32)
    mod_scr = dscr("mod_scr", [2, 6, 128, 1024], F32)

    groups = []
    groups.append(dict(name="S", samp=True, cond=0, tb=0, x=xs, nt=NT_S, full=list(range(FULL_S)), own=16,
                       sbase=0, nkt=36, kofs=0, rr=rope_rs, ra=rope_att))
    for p in range(2):
        groups.append(dict(name="P%d" % p, samp=False, cond=1, tb=1, x=xp[p * 256:(p + 1) * 256, :], nt=2,
                           full=[0, 1], own=2, sbase=FULL_S + 2 * p, nkt=2, kofs=0, rr=rope_rp, ra=None, p=p))

    with contextlib.ExitStack() as top:
        esems = {k: top.enter_context(nc.semaphore("sem_" + k)) for k in ENG_ATTR}
        dsems = [top.enter_context(nc.semaphore("dsem%d" % i)) for i in range(NDSEM)]
        S = Sched(nc, esems, dsems)
        slotbase = [0.0]

        def V(fn, r=(), w=()):
            S.op("dve", fn, r, w)

        def A(fn, r=(), w=()):
            S.op("act", fn, r, w)

        def G(fn, r=(), w=()):
            S.op("pool", fn, r, w)

        def T(fn, r=(), w=()):
            S.op("pe", fn, r, w)

        def DS(fn, r=(), w=(), sk=None):
            S.op("sp", fn, r, w, dma=True, sk=sk)

        def DG(fn, r=(), w=()):
            S.op("pool", fn, r, w, dma=True)

        def psb(name, shape, dt):
            return top.enter_context(nc.sbuf_tensor(name, list(shape), dt))

        identf = psb("identf", [128, 128], F32)
        ident = psb("ident", [128, 128], BF16)
        ones_bf = psb("ones_bf", [128, 128], BF16)
        negC = psb("negC", [128, 1], F32)
        QDF = [psb("QDF%d" % i, [128, 8], F32) for i in range(2)]
        QDB = [psb("QDB%d" % i, [128, 8], F32) for i in range(2)]
        KDF = [psb("KDF%d" % i, [128, 8], F32) for i in range(2)]
        KDB = [psb("KDB%d" % i, [128, 8], F32) for i in range(2)]
        CDF = [psb("CDF%d" % i, [128, 8], F32) for i in range(2)]
        CDB = [psb("CDB%d" % i, [128, 8], F32) for i in range(2)]
        qgb = psb("qgb", [128, 128], F32)
        kgb = psb("kgb", [128, 128], F32)
        sbst = contextlib.ExitStack()
        MT = [sbst.enter_context(nc.sbuf_tensor("MT%d" % i, [128, 1024], F32)) for i in range(2)]
        SBs = sbst.enter_context(nc.sbuf_tensor("SBs", [128, FULL_S + 4, 512], BF16))
        kvst = contextlib.ExitStack()
        KT = kvst.enter_context(nc.sbuf_tensor("KT", [128, 2, 36 * 128], BF16))
        Vb = kvst.enter_context(nc.sbuf_tensor("Vb", [128, 36, 256], BF16))
        KTp = [kvst.enter_context(nc.sbuf_tensor("KTp%d" % p, [128, 2, 256], BF16)) for p in range(2)]
        Vp = [kvst.enter_context(nc.sbuf_tensor("Vp%d" % p, [128, 2, 256], BF16)) for p in range(2)]

        def scr_tile(scr, idx):
            return scr[idx]

        with contextlib.ExitStack() as st:
            def sb(name, shape, dt):
                return st.enter_context(nc.sbuf_tensor(name, list(shape), dt))

            def ps(name, shape, dt):
                return st.enter_context(nc.psum_tensor(name, list(shape), dt))

            tabs_t = sb("tabs_t", [128, 4, 128], F32)
            tabe_t = sb("tabe_t", [128, 4], F32)
            decb = sb("decb", [128, 32], F32)
            lg = sb("lg", [128, 32], F32)
            tmp1 = sb("tmp1", [128, 128], F32)
            tmp2 = sb("tmp2", [128, 128], F32)
            tsm = sb("tsm", [128, 8], F32)
            cTt = sb("cTt", [128, 8, 2], F32)
            scb = [sb("scb%d" % j, [128, 8, 128], BF16) for j in range(2)]
            bmb = sb("bmb", [128, 6144], F32)
            n1gb = sb("n1gb", [128, D], F32)
            n2gb = sb("n2gb", [128, D], F32)
            wm = [sb("wm%d" % i, [128, 8, 512], BF16) for i in range(2)]
            mtmp = [sb("mtmp%d" % i, [128, 512], F32) for i in range(4)]
            ckt = sb("ckt", [128, 8, 128], F32)
            cvt = sb("cvt", [128, 8, 128], F32)
            ckb = sb("ckb", [128, 8, 128], BF16)
            sq = sb("sq", [128, 8, 128], F32)
            cs = sb("cs", [128, 8], F32)
            cm = sb("cm", [128, 4], F32)
            row = sb("row", [1, 128], F32)
            onesf = sb("onesf", [1, 128], F32)
            pM = [ps("pM%d" % i, [128, 512], F32) for i in range(2)]
            pTk = ps("pTk", [128, 8, 128], BF16)
            pR = ps("pR", [1, 128], F32)
            pC = ps("pC", [128, 1], F32)

            DS(lambda e: e.dma_start(out=tabs_t[:], in_=tabs), w=["tabs"])
            DS(lambda e: e.dma_start(out=tabe_t[:], in_=tabe), w=["tabe"])
            DS(lambda e: e.dma_start(out=decb[:], in_=dec.partition_broadcast(128)), w=["decb"])
            DS(lambda e: e.dma_start(out=cTt[:], in_=cT), w=["cTt"])
            DS(lambda e: e.dma_start(out=qgb[:], in_=qg.partition_broadcast(128)), w=["qgb"])
            DS(lambda e: e.dma_start(out=kgb[:], in_=kg.partition_broadcast(128)), w=["kgb"])
            DS(lambda e: e.dma_start(out=bmb[:], in_=b_mod.partition_broadcast(128)), w=["bmb"])
            DS(lambda e: e.dma_start(out=n1gb[:], in_=n1g.partition_broadcast(128)), w=["n1gb"])
            DS(lambda e: e.dma_start(out=n2gb[:], in_=n2g.partition_broadcast(128)), w=["n2gb"])
            DS(lambda e: e.dma_start(out=ckt[:].rearrange("p (k t) d -> p k t d", k=2),
                                     in_=ck.rearrange("k (t p) d -> p k t d", p=128)), w=["ckt"])
            DS(lambda e: e.dma_start(out=cvt[:].rearrange("p (k t) d -> p k t d", k=2),
                                     in_=cv.rearrange("k (t p) d -> p k t d", p=128)), w=["cvt"])
            G(lambda e: e.memset(identf[:], 0.0), w=["identf"])
            G(lambda e: e.affine_select(out=identf[:], in_=identf[:], pattern=[[-1, 128]], compare_op=ALU.not_equal,
                                        fill=1.0, base=0, channel_multiplier=1), r=["identf"], w=["identf"])
            V(lambda e: e.tensor_copy(out=ident[:], in_=identf[:]), r=["identf"], w=["ident"])
            G(lambda e: e.memset(ones_bf[:], 1.0), w=["ones_bf"])
            G(lambda e: e.memset(onesf[:], 1.0), w=["onesf"])
            A(lambda e: e.activation(out=lg[:], in_=decb[:], func=AF.Exp, scale=-1.0), r=["decb"], w=["lg"])
            A(lambda e: e.activation(out=lg[:], in_=lg[:], func=AF.Ln, bias=1.0), r=["lg"], w=["lg"])
            V(lambda e: e.tensor_scalar(out=lg[:], in0=lg[:], scalar1=-1.0, scalar2=None, op0=ALU.mult), r=["lg"], w=["lg"])
            for tb in range(2):
                lf = lambda h, tb=tb: lg[:, 16 * tb + h:16 * tb + h + 1]
                lb = lambda h, tb=tb: lg[:, 16 * tb + 8 + h:16 * tb + 8 + h + 1]
                for h in range(8):
                    A(lambda e, h=h, lf=lf: e.activation(out=tmp1[:], in_=tabs_t[:, 0, :], func=AF.Exp, scale=lf(h)),
                      r=["tabs", "lg", "tmp1"], w=["tmp1"])
                    V(lambda e: e.tensor_tensor(out=tmp1[:], in0=tmp1[:], in1=tabs_t[:, 1, :], op=ALU.mult), r=["tmp1", "tabs"], w=["tmp1"])
                    A(lambda e, h=h, lb=lb: e.activation(out=tmp2[:], in_=tabs_t[:, 2, :], func=AF.Exp, scale=lb(h)),
                      r=["tabs", "lg", "tmp2"], w=["tmp2"])
                    V(lambda e: e.tensor_tensor(out=tmp2[:], in0=tmp2[:], in1=tabs_t[:, 3, :], op=ALU.mult), r=["tmp2", "tabs"], w=["tmp2"])
                    V(lambda e: e.tensor_tensor(out=tmp1[:], in0=tmp1[:], in1=tmp2[:], op=ALU.add), r=["tmp1", "tmp2"], w=["tmp1"])
                    V(lambda e, h=h, tb=tb: e.tensor_scalar(out=MT[tb][:, h * 128:(h + 1) * 128], in0=tmp1[:], scalar1=0.125, scalar2=None,
                                                            op0=ALU.mult), r=["tmp1"], w=["MT"])
                lgf = lg[:, 16 * tb:16 * tb + 8]
                lgb = lg[:, 16 * tb + 8:16 * tb + 16]
                for (dst, src, col, mul) in ((QDF, lgf, 0, 1.0), (QDB, lgb, 1, 1.0), (KDF, lgf, 2, 0.125), (KDB, lgb, 3, 0.125)):
                    V(lambda e, src=src, col=col: e.tensor_scalar(out=tsm[:], in0=src, scalar1=tabe_t[:, col:col + 1], scalar2=None, op0=ALU.mult),
                      r=["lg", "tabe", "tsm"], w=["tsm"])
                    A(lambda e, dst=dst, tb=tb: e.activation(out=dst[tb][:], in_=tsm[:], func=AF.Exp), r=["tsm"], w=["dtab"])
                    if mul != 1.0:
                        V(lambda e, dst=dst, tb=tb, mul=mul: e.tensor_scalar(out=dst[tb][:], in0=dst[tb][:], scalar1=mul, scalar2=None, op0=ALU.mult),
                          r=["dtab"], w=["dtab"])
                A(lambda e, tb=tb, lgf=lgf: e.activation(out=CDF[tb][:], in_=lgf, func=AF.Exp, scale=128.0), r=["lg"], w=["cd"])
                A(lambda e, tb=tb, lgb=lgb: e.activation(out=CDB[tb][:], in_=lgb, func=AF.Exp, scale=128.0), r=["lg"], w=["cd"])
            V(lambda e: e.tensor_tensor(out=sq[:], in0=ckt[:], in1=ckt[:], op=ALU.mult), r=["ckt"], w=["sq"])
            V(lambda e: e.tensor_reduce(out=cs[:], in_=sq[:], axis=AX.X, op=ALU.add), r=["sq"], w=["cs"])
            V(lambda e: e.tensor_reduce(out=cm[:, 0:1], in_=cs[:], axis=AX.X, op=ALU.max), r=["cs"], w=["cm0"])
            T(lambda e: e.transpose(out=pR[:], in_=cm[:, 0:1], identity=identf[:]), r=["cm0", "identf"], w=["pR"])
            V(lambda e: e.tensor_reduce(out=row[:, 0:1], in_=pR[:], axis=AX.X, op=ALU.max), r=["pR"], w=["row"])
            T(lambda e: e.matmul(pC[:], lhsT=onesf[:], rhs=row[:, 0:1], start=True, stop=True), r=["onesf", "row"], w=["pC"])
            A(lambda e: e.activation(out=cm[:, 1:2], in_=pC[:], func=AF.Sqrt), r=["pC"], w=["cm1"])
            V(lambda e: e.tensor_reduce(out=cm[:, 2:3], in_=kgb[:], axis=AX.X, op=ALU.max, apply_absolute_value=True), r=["kgb"], w=["cm2"])
            V(lambda e: e.tensor_scalar(out=cm[:, 2:3], in0=cm[:, 2:3], scalar1=float(np.sqrt(128.0)), scalar2=None, op0=ALU.mult), r=["cm2"], w=["cm2"])
            V(lambda e: e.tensor_tensor(out=cm[:, 1:2], in0=cm[:, 1:2], in1=cm[:, 2:3], op=ALU.max), r=["cm1", "cm2"], w=["cm1"])
            V(lambda e: e.tensor_reduce(out=cm[:, 3:4], in_=qgb[:], axis=AX.X, op=ALU.max, apply_absolute_value=True), r=["qgb"], w=["cm3"])
            V(lambda e: e.tensor_tensor(out=cm[:, 1:2], in0=cm[:, 1:2], in1=cm[:, 3:4], op=ALU.mult), r=["cm1", "cm3"], w=["cm1"])
            V(lambda e: e.tensor_scalar(out=negC[:], in0=cm[:, 1:2], scalar1=-1.0, scalar2=None, op0=ALU.mult), r=["cm1"], w=["negC"])
            V(lambda e: e.tensor_copy(out=ckb[:], in_=ckt[:]), r=["ckt"], w=["ckb"])
            for j in range(8):
                T(lambda e, j=j: e.transpose(out=pTk[:, j, :], in_=ckb[:, j, :], identity=ident[:]), r=["ckb", "ident"], w=["pTk"])
            A(lambda e: e.copy(out=KT[:, :, 32 * 128:36 * 128], in_=pTk[:].rearrange("p (k t) d -> p k (t d)", k=2)), r=["pTk"], w=["KTc"])
            for kvh in range(2):
                V(lambda e, kvh=kvh: e.tensor_copy(out=Vb[:, 32:36, kvh * 128:(kvh + 1) * 128], in_=cvt[:, kvh * 4:(kvh + 1) * 4, :]), r=["cvt"], w=["Vbc"])
            A(lambda e: e.activation(out=cTt[:], in_=cTt[:], func=AF.Silu), r=["cTt"], w=["cTt"])
            for j in range(2):
                V(lambda e, j=j: e.tensor_copy(out=scb[j][:], in_=cTt[:, :, j:j + 1].to_broadcast([128, 8, 128])), r=["cTt"], w=["scb%d" % j])
            for cg in range(12):
                wbuf, wkey = wm[cg % 2], "wm%d" % (cg % 2)
                DG(lambda e, cg=cg, wbuf=wbuf: e.dma_start(out=wbuf[:], in_=w_mod[:, :, cg * 512:(cg + 1) * 512]), w=[wkey])
                sec, half = cg // 2, cg % 2
                for j in range(2):
                    for k in range(8):
                        T(lambda e, j=j, k=k, wbuf=wbuf: e.matmul(pM[j][:], lhsT=scb[j][:, k, :], rhs=wbuf[:, k, :], start=(k == 0), stop=(k == 7)),
                          r=["scb%d" % j, wkey], w=["pM%d" % j])
                    mt_, mkey = mtmp[(2 * cg + j) % 4], "mtmp%d" % ((2 * cg + j) % 4)
                    V(lambda e, j=j, cg=cg, mt_=mt_: e.tensor_tensor(out=mt_[:], in0=pM[j][:], in1=bmb[:, cg * 512:(cg + 1) * 512], op=ALU.add),
                      r=["pM%d" % j, "bmb"], w=[mkey])
                    if sec in (1, 4):
                        gsrc, gk = (n1gb, "n1gb") if sec == 1 else (n2gb, "n2gb")
                        V(lambda e, mt_=mt_, gsrc=gsrc, half=half: e.scalar_tensor_tensor(out=mt_[:], in0=mt_[:], scalar=1.0, in1=gsrc[:, half * 512:(half + 1) * 512],
                                                                                         op0=ALU.add, op1=ALU.mult), r=[mkey, gk], w=[mkey])
                    DS(lambda e, j=j, sec=sec, half=half, mt_=mt_: e.dma_start(out=mod_scr[j, sec, :, half * 512:(half + 1) * 512], in_=mt_[:]),
                       r=[mkey], w=["mod_scr"])
            S.emit_phase()

        with contextlib.ExitStack() as st:
            def sb(name, shape, dt):
                return st.enter_context(nc.sbuf_tensor(name, list(shape), dt))

            def ps(name, shape, dt):
                return st.enter_context(nc.psum_tensor(name, list(shape), dt))

            wA = sb("wA", [128, 8, 2048], BF16)
            Gb = [sb("Gb%d" % j, [128, D], F32) for j in range(2)]
            shb = [sb("shb%d" % j, [128, D], F32) for j in range(2)]
            xt_r = Rot([sb("xtA%d" % i, [128, D], F32) for i in range(4)], "xtA")
            ra_r = Rot([sb("raA%d" % i, [128, 2, 128], F32) for i in range(6)], "raA")
            rr_r = Rot([sb("rrA%d" % i, [128, 2, 64], F32) for i in range(6)], "rrA")
            junk = sb("junkA", [128, D], F32)
            ss_r = Rot([sb("ssA%d" % i, [128, 4], F32) for i in range(4)], "ssA")
            tmpf_r = Rot([sb("tmpfA%d" % i, [128, D], F32) for i in range(2)], "tmpfA")
            hb_r = Rot([sb("hbA%d" % i, [128, D], BF16) for i in range(3)], "hbA")
            hT_r = Rot([sb("hTA%d" % i, [128, 8, 128], BF16) for i in range(2)], "hTA")
            knr_r = Rot([sb("knA%d" % i, [128, 2, 128], F32) for i in range(2)], "knA")
            kn_r = Rot([sb("knoA%d" % i, [128, 2, 128], F32) for i in range(2)], "knoA")
            vf_r = Rot([sb("vfA%d" % i, [128, 256], F32) for i in range(2)], "vfA")
            tA_r = Rot([sb("tAA%d" % i, [128, 8, 64], F32) for i in range(4)], "tAA")
            tB_r = Rot([sb("tBA%d" % i, [128, 8, 64], F32) for i in range(4)], "tBA")
            kb_r = Rot([sb("kbA%d" % i, [128, 2, 128], BF16) for i in range(3)], "kbA")
            rkr_r = Rot([sb("rkrA%d" % i, [128, 8, 64], F32) for i in range(2)], "rkrA")
            kdb_r = Rot([sb("kdbA%d" % i, [128, 8, 64], BF16) for i in range(3)], "kdbA")
            rvb_r = Rot([sb("rvbA%d" % i, [128, 1024], BF16) for i in range(3)], "rvbA")
            Sb = sb("SbA", [128, 4, 128], F32)
            pT_r = Rot([ps("pTA%d" % i, [128, 8, 128], BF16) for i in range(1)], "pTA")
            pP = Rot([ps("pPA%d" % i, [128, 512], F32) for i in range(4)], "pPA")
            pT2 = ps("pT2A", [128, 8, 128], BF16)
            pKV = [ps("pKVA%d" % i, [128, 4, 128], F32) for i in range(2)]

            DG(lambda e: e.dma_start(out=wA[:, :, 0:512], in_=w_in[:, :, 1024:1536]), w=["wA0"])
            for i in range(3):
                DG(lambda e, i=i: e.dma_start(out=wA[:, :, 512 * (i + 1):512 * (i + 2)], in_=w_in[:, :, 2048 + 512 * i:2048 + 512 * (i + 1)]), w=["wA%d" % (i + 1)])
            for j in range(2):
                DS(lambda e, j=j: e.dma_start(out=Gb[j][:], in_=mod_scr[j, 1]), w=["Gb"])
                DS(lambda e, j=j: e.dma_start(out=shb[j][:], in_=mod_scr[j, 0]), w=["shb"])

            def _grp(g):
                samp, tb, cond = g["samp"], g["tb"], g["cond"]
                KTg = KT if samp else KTp[g["p"]]
                order = list(reversed(range(g["nt"])))
                loaded = {}

                def load(t, g=g, samp=samp, loaded=loaded):
                    xt, xk = xt_r.next()
                    DS(lambda e, t=t, xt=xt: e.dma_start(out=xt[:], in_=g["x"][t * 128:(t + 1) * 128, :]), w=[xk])
                    rr, rrk = rr_r.next()
                    DS(lambda e, t=t, rr=rr: e.dma_start(out=rr[:], in_=g["rr"][t * 128:(t + 1) * 128]), w=[rrk])
                    ra, rak = None, None
                    if samp:
                        ra, rak = ra_r.next()
                        DS(lambda e, t=t, ra=ra: e.dma_start(out=ra[:], in_=g["ra"][t * 128:(t + 1) * 128]), w=[rak])
                    loaded[t] = (xt, xk, rr, rrk, ra, rak)

                if samp:
                    s0v = s0b.rearrange("p (a b) d -> p a b d", b=2)
                    DS(lambda e: e.dma_start(out=Sb[0:64], in_=s0v[:, :, 0, :]), w=["Sb"])
                    DS(lambda e: e.dma_start(out=Sb[64:128], in_=s0v[:, :, 1, :]), w=["Sb"])
                else:
                    G(lambda e: e.memset(Sb[:], 0.0), w=["Sb"])
                base = slotbase[0]
                slotbase[0] += len(order) + 8
                for oi, t in enumerate(order):
                    S.slot = base + oi - 2
                    load(t)
                    S.slot = base + oi
                    xt, xk, rr, rrk, ra, rak = loaded.pop(t)
                    full = t in g["full"]
                    ss, ssk = ss_r.next()
                    tmpf, tmpfk = tmpf_r.next()
                    A(lambda e, xt=xt, ss=ss: e.activation(out=junk[:], in_=xt[:], func=AF.Square, scale=float(D ** -0.5), accum_out=ss[:, 0:1]), r=[xk], w=[ssk + "a"])
                    A(lambda e, ss=ss: e.activation(out=ss[:, 0:1], in_=ss[:, 0:1], func=AF.Sqrt, bias=EPS), r=[ssk + "a"], w=[ssk + "a"])
                    V(lambda e, ss=ss: e.reciprocal(out=ss[:, 0:1], in_=ss[:, 0:1]), r=[ssk + "a"], w=[ssk + "a"])
                    S.slot = base + oi + 0.5
                    V(lambda e, xt=xt, cond=cond, ss=ss, tmpf=tmpf: e.scalar_tensor_tensor(out=tmpf[:], in0=xt[:], scalar=ss[:, 0:1], in1=Gb[cond][:], op0=ALU.mult, op1=ALU.mult),
                      r=[xk, ssk + "a", "Gb"], w=[tmpfk])
                    hb, hbk = hb_r.next()
                    G(lambda e, hb=hb, cond=cond, tmpf=tmpf: e.tensor_tensor(out=hb[:], in0=tmpf[:], in1=shb[cond][:], op=ALU.add), r=[tmpfk, "shb"], w=[hbk])
                    S.slot = base + oi + 1
                    pT, pTk = pT_r.next()
                    for k in range(8):
                        T(lambda e, k=k, hb=hb, pT=pT: e.transpose(out=pT[:, k, :], in_=hb[:, k * 128:(k + 1) * 128], identity=ident[:]), r=[hbk], w=[pTk])
                    hT, hTk = hT_r.next()
                    A(lambda e, hT=hT, pT=pT: e.copy(out=hT[:], in_=pT[:]), r=[pTk], w=[hTk])
                    if full:
                        si = g["sbase"] + g["full"].index(t)
                        DS(lambda e, hT=hT, si=si: e.dma_start(out=hT_scr[si], in_=hT[:].rearrange("p k d -> p (k d)")), r=[hTk], w=["hT_scr"])
                    pps = []
                    for cgi in range(4):
                        S.slot = base + oi + 2
                        pp, ppk = pP.next()
                        for k in range(8):
                            T(lambda e, k=k, cgi=cgi, pp=pp, hT=hT: e.matmul(pp[:], lhsT=hT[:, k, :], rhs=wA[:, k, cgi * 512:(cgi + 1) * 512], start=(k == 0), stop=(k == 7)),
                              r=[hTk, "wA%d" % cgi], w=[ppk])
                        pps.append((pp, ppk))
                        S.slot = base + oi + 2
                        if cgi == 0:
                            for hh in range(2):
                                A(lambda e, hh=hh, pp=pp, ss=ss: e.activation(out=junk[:, 0:128], in_=pp[:, hh * 128:(hh + 1) * 128], func=AF.Square, scale=float(128 ** -0.5),
                                                                           accum_out=ss[:, 1 + hh:2 + hh]), r=[ppk], w=[ssk + "b"])
                            A(lambda e, ss=ss: e.activation(out=ss[:, 1:3], in_=ss[:, 1:3], func=AF.Sqrt, bias=EPS), r=[ssk + "b"], w=[ssk + "b"])
                            V(lambda e, ss=ss: e.reciprocal(out=ss[:, 1:3], in_=ss[:, 1:3]), r=[ssk + "b"], w=[ssk + "b"])
                            kb, kbk = kb_r.next()
                            if samp:
                                kn, knk = knr_r.next()
                                for hh in range(2):
                                    V(lambda e, hh=hh, pp=pp, ss=ss, kn=kn: e.scalar_tensor_tensor(out=kn[:, hh, :], in0=pp[:, hh * 128:(hh + 1) * 128], scalar=ss[:, 1 + hh:2 + hh],
                                                                                                   in1=kgb[:], op0=ALU.mult, op1=ALU.mult), r=[ppk, ssk + "b", "kgb"], w=[knk])
                                tA, tAk = tA_r.next()
                                tB, tBk = tB_r.next()
                                S.slot = base + oi + 3
                                rope_att_ops(V, G, kn, knk, ra, rak, kb, kbk, 2, tA, tB, tAk, tBk)
                                S.slot = base + oi + 2
                            else:
                                kno, knok = kn_r.next()
                                for hh in range(2):
                                    V(lambda e, hh=hh, pp=pp, kno=kno, ss=ss: e.scalar_tensor_tensor(out=kno[:, hh, :], in0=pp[:, hh * 128:(hh + 1) * 128], scalar=ss[:, 1 + hh:2 + hh],
                                                                                                     in1=kgb[:], op0=ALU.mult, op1=ALU.mult), r=[ppk, ssk + "b", "kgb"], w=[knok])
                                DS(lambda e, kno=kno, t=t, g=g: e.dma_start(out=nk[g["p"], :, t * 128:(t + 1) * 128, :].rearrange("k p d -> p k d"), in_=kno[:]), r=[knok], w=["nk"])
                                S.slot = base + oi + 3
                                G(lambda e, kb=kb, kno=kno: e.tensor_copy(out=kb[:], in_=kno[:]), r=[knok], w=[kbk])
                            S.slot = base + oi + 4
                            for hh in range(2):
                                T(lambda e, hh=hh, kb=kb: e.transpose(out=pT2[:, hh, :], in_=kb[:, hh, :], identity=ident[:]), r=[kbk], w=["pT2A"])
                            A(lambda e, t=t, KTg=KTg: e.copy(out=KTg[:, :, t * 128:(t + 1) * 128], in_=pT2[:, 0:2, :]), r=["pT2A"], w=["KT"])
                            S.slot = base + oi + 2
                            if samp:
                                A(lambda e, t=t, pp=pp: e.copy(out=Vb[:, t, :], in_=pp[:, 256:512]), r=[ppk], w=["Vb"])
                            else:
                                vf, vfk = vf_r.next()
                                A(lambda e, vf=vf, pp=pp: e.copy(out=vf[:], in_=pp[:, 256:512]), r=[ppk], w=[vfk])
                                DS(lambda e, vf=vf, t=t, g=g: e.dma_start(out=nv[g["p"], :, t * 128:(t + 1) * 128, :].rearrange("k p d -> p k d"),
                                                                          in_=vf[:].rearrange("p (k d) -> p k d", k=2)), r=[vfk], w=["nv"])
                                G(lambda e, vf=vf, t=t, g=g: e.tensor_copy(out=Vp[g["p"]][:, t, :], in_=vf[:]), r=[vfk], w=["Vb"])
                    (pk, pkk), (pv0, pv0k), (pv1, pv1k) = pps[1], pps[2], pps[3]
                    rkr, rkrk = rkr_r.next()
                    tA, tAk = tA_r.next()
                    tB, tBk = tB_r.next()
                    rope_ret_ops(V, G, pk[:].rearrange("p (h d) -> p h d", h=8), pkk, rr, rrk, rkr, rkrk, tA, tB, tAk, tBk)
                    kdb, kdbk = kdb_r.next()
                    S.slot = base + oi + 3
                    V(lambda e, kdb=kdb, tb=tb, rkr=rkr: e.tensor_tensor(out=kdb[:], in0=rkr[:], in1=KDB[tb][:].unsqueeze(2).to_broadcast([128, 8, 64]), op=ALU.mult),
                      r=[rkrk], w=[kdbk])
                    S.slot = base + oi + 2
                    rvb, rvbk = rvb_r.next()
                    A(lambda e, rvb=rvb, pv0=pv0: e.copy(out=rvb[:, 0:512], in_=pv0[:]), r=[pv0k], w=[rvbk + "a"])
                    A(lambda e, rvb=rvb, pv1=pv1: e.copy(out=rvb[:, 512:1024], in_=pv1[:]), r=[pv1k], w=[rvbk + "b"])
                    S.slot = base + oi + 4
                    if full:
                        slot = g["sbase"] + g["full"].index(t)
                        G(lambda e, slot=slot: e.tensor_copy(out=SBs[:, slot, :], in_=Sb[:].rearrange("p a d -> p (a d)")), r=["Sb"], w=["SBs"])
                    last = (oi == len(order) - 1)
                    if samp and last:
                        continue
                    for h in range(8):
                        T(lambda e, h=h, kdb=kdb, rvb=rvb: e.matmul(pKV[h // 4][:, h % 4, :], lhsT=kdb[:, 2 * (h // 2):2 * (h // 2) + 2, :].rearrange("p a d -> p (a d)"),
                                                                    rhs=rvb[:, h * 128:(h + 1) * 128], start=True, stop=True),
                          r=[kdbk, rvbk + "a", rvbk + "b"], w=["pKVA%d" % (h // 4)])
                    cdv = CDB[tb][:].rearrange("p (a b) -> p a b", b=2)
                    for par in range(2):
                        lo = 64 * par
                        V(lambda e, par=par, lo=lo, cdv=cdv: e.tensor_tensor(out=Sb[lo:lo + 64], in0=Sb[lo:lo + 64],
                                                                              in1=cdv[lo:lo + 64, :, par].unsqueeze(2).to_broadcast([64, 4, 128]), op=ALU.mult), r=["Sb"], w=["Sb"])
                        for bk in range(2):
                            V(lambda e, par=par, lo=lo, bk=bk: e.tensor_tensor(out=Sb[lo:lo + 64, 2 * bk:2 * bk + 2, :], in0=Sb[lo:lo + 64, 2 * bk:2 * bk + 2, :],
                                                                                in1=pKV[bk][lo:lo + 64].rearrange("p (a b) d -> p a b d", b=2)[:, :, par, :], op=ALU.add),
                              r=["Sb", "pKVA%d" % bk], w=["Sb"])
                    if (not samp) and last:
                        nv_ = nsb[g["p"]].rearrange("p (a b) d -> p a b d", b=2)
                        for par in range(2):
                            DS(lambda e, par=par, nv_=nv_: e.dma_start(out=nv_[:, :, par, :], in_=Sb[64 * par:64 * par + 64]), r=["Sb"], w=["nsb"])
            for g_ in groups:
                _grp(g_)
            S.emit_phase()

        if flags.get("stop_after") == "A":
            kvst.close()
            sbst.close()
            return nc

        with contextlib.ExitStack() as st:
            def sb(name, shape, dt):
                return st.enter_context(nc.sbuf_tensor(name, list(shape), dt))

            def ps(name, shape, dt):
                return st.enter_context(nc.psum_tensor(name, list(shape), dt))

            wq = sb("wq", [128, 8, 1024], BF16)
            slotbase[0] = 0.0
            hTb_r = Rot([sb("hTbB%d" % i, [128, 4, 1024], BF16) for i in range(2)], "hTbB")
            raB_r = Rot([sb("raB%d" % i, [128, 4, 2, 128], F32) for i in range(2)], "raB")
            qT_r = Rot([sb("qTB%d" % i, [128, 8, 512], BF16) for i in range(2)], "qTB")
            ya_r = Rot([sb("yaB%d" % i, [128, 8, 512], BF16) for i in range(2)], "yaB")
            junk = sb("junkB", [128, 128], F32)
            ssq_r = Rot([sb("ssqB%d" % i, [128, 4], F32) for i in range(3)], "ssqB")
            qn_r = Rot([sb("qnB%d" % i, [128, 4, 128], F32) for i in range(2)], "qnB")
            qraw_r = Rot([sb("qrawB%d" % i, [128, 512], F32) for i in range(3)], "qrawB")
            tA_r = Rot([sb("tAB%d" % i, [128, 8, 64], F32) for i in range(2)], "tAB")
            tB_r = Rot([sb("tBB%d" % i, [128, 8, 64], F32) for i in range(2)], "tBB")
            qb_r = Rot([sb("qbB%d" % i, [128, 4, 128], BF16) for i in range(3)], "qbB")
            pe_r = Rot([sb("peB%d" % i, [128, 512], BF16) for i in range(6)], "peB")
            rs_r = Rot([sb("rsB%d" % i, [128, 512], F32) for i in range(2)], "rsB")
            pQ = ps("pQB", [128, 512], F32)
            pTq = ps("pTqB", [128, 8, 128], BF16)
            pS_r = Rot([ps("pSB%d" % i, [128, 512], F32) for i in range(3)], "pSB")
            pO_r = Rot([ps("pOB%d" % i, [128, 512], F32) for i in range(2)], "pOB")
            pZ_r = Rot([ps("pZB%d" % i, [128, 512], F32) for i in range(1)], "pZB")
            for i in range(2):
                DG(lambda e, i=i: e.dma_start(out=wq[:, :, i * 512:(i + 1) * 512], in_=w_in[:, :, i * 512:(i + 1) * 512]), w=["wq%d" % i])
            def _grp(g):
                samp = g["samp"]
                KTg = KT if samp else KTp[g["p"]]
                Vg = Vb if samp else Vp[g["p"]]
                nkt = g["nkt"]
                blocks = [g["full"][i:i + 4] for i in range(0, len(g["full"]), 4)]
                def _blk(blk):
                    bi = int(slotbase[0])
                    slotbase[0] += 1
                    nq = 128 * len(blk)
                    S.slot = 10 * bi - 8
                    hTb, hTbk = hTb_r.next()
                    rab, rabk = raB_r.next()
                    for j, t in enumerate(blk):
                        si = g["sbase"] + g["full"].index(t)
                        DS(lambda e, j=j, si=si, hTb=hTb: e.dma_start(out=hTb[:, j, :], in_=hT_scr[si]), w=[hTbk + "_%d" % j])
                        if samp:
                            DS(lambda e, j=j, t=t, rab=rab: e.dma_start(out=rab[:, j], in_=g["ra"][t * 128:(t + 1) * 128]), w=[rabk + "_%d" % j])
                    qT, qTk = qT_r.next()
                    for j, t in enumerate(blk):
                        for hf in range(2):
                            u = 2 * j + hf
                            S.slot = 10 * bi + u + 0.5
                            ssq, ssqk = ssq_r.next()
                            qn, qnk = qn_r.next()
                            for k in range(8):
                                T(lambda e, j=j, hf=hf, k=k, hTb=hTb: e.matmul(pQ[:], lhsT=hTb[:, j, k * 128:(k + 1) * 128], rhs=wq[:, k, hf * 512:(hf + 1) * 512],
                                                                               start=(k == 0), stop=(k == 7)), r=[hTbk + "_%d" % j, "wq%d" % hf], w=["pQB"])
                            qraw, qrawk = qraw_r.next()
                            V(lambda e, qraw=qraw: e.tensor_copy(out=qraw[:], in_=pQ[:]), r=["pQB"], w=[qrawk])
                            for hh in range(4):
                                V(lambda e, hh=hh, ssq=ssq, qraw=qraw: e.scalar_tensor_tensor(out=junk[:], in0=qraw[:, hh * 128:(hh + 1) * 128], scalar=1.0 / 128, in1=qraw[:, hh * 128:(hh + 1) * 128],
                                                                                              op0=ALU.mult, op1=ALU.mult, accum_out=ssq[:, hh:hh + 1]), r=[qrawk], w=[ssqk])
                            S.slot = 10 * bi + u + 1.5
                            G(lambda e, ssq=ssq: e.tensor_scalar(out=ssq[:], in0=ssq[:], scalar1=EPS, scalar2=None, op0=ALU.add), r=[ssqk], w=[ssqk])
                            A(lambda e, ssq=ssq: e.activation(out=ssq[:], in_=ssq[:], func=AF.Ln), r=[ssqk], w=[ssqk])
                            A(lambda e, ssq=ssq: e.activation(out=ssq[:], in_=ssq[:], func=AF.Exp, scale=-0.5), r=[ssqk], w=[ssqk])
                            for hh in range(4):
                                V(lambda e, hh=hh, ssq=ssq, qn=qn, qraw=qraw: e.scalar_tensor_tensor(out=qn[:, hh, :], in0=qraw[:, hh * 128:(hh + 1) * 128], scalar=ssq[:, hh:hh + 1], in1=qgb[:],
                                                                                                     op0=ALU.mult, op1=ALU.mult), r=[qrawk, ssqk], w=[qnk])
                            qb, qbk = qb_r.next()
                            if samp:
                                tA, tAk = tA_r.next()
                                tB, tBk = tB_r.next()
                                rope_att_ops(V, G, qn, qnk, rab[:, j], rabk + "_%d" % j, qb, qbk, 4, tA, tB, tAk, tBk)
                            else:
                                G(lambda e, qb=qb, qn=qn: e.tensor_copy(out=qb[:], in_=qn[:]), r=[qnk], w=[qbk])
                            S.slot = 10 * bi + u + 2.5
                            for hh in range(4):
                                T(lambda e, hh=hh, qb=qb: e.transpose(out=pTq[:, hh, :], in_=qb[:, hh, :], identity=ident[:]), r=[qbk], w=["pTqB"])
                            V(lambda e, j=j, hf=hf, qT=qT: e.tensor_copy(out=qT[:, hf * 4:(hf + 1) * 4, j * 128:(j + 1) * 128], in_=pTq[:, 0:4, :]), r=["pTqB"], w=[qTk + "_%d" % hf])
                    nqa = 32 if (samp and len(blk) == 1) else nq
                    ya, yak = ya_r.next()
                    for h in range(8):
                        S.slot = 10 * (bi + 1) + h
                        kvh = h // 4
                        pO, pOk = pO_r.next()
                        pZ, pZk = pZ_r.next()
                        LA = 2
                        pending = []
                        for kt in range(nkt + LA):
                            if kt < nkt:
                                pS, pSk = pS_r.next()
                                T(lambda e, kt=kt, pS=pS, qT=qT, h=h, kvh=kvh: e.matmul(pS[:, 0:nqa], lhsT=KTg[:, kvh, kt * 128:(kt + 1) * 128], rhs=qT[:, h, 0:nqa],
                                                                                        start=True, stop=True), r=[qTk + "_%d" % (h // 4), "KT"], w=[pSk])
                                pe, pek = pe_r.next()
                                A(lambda e, pS=pS, pe=pe: e.activation(out=pe[:, 0:nqa], in_=pS[:, 0:nqa], func=AF.Exp, scale=SCALE, bias=negC[:, 0:1]), r=[pSk], w=[pek])
                                pending.append((kt, pe, pek))
                            if kt >= LA:
                                k0, pe0, pek0 = pending.pop(0)
                                T(lambda e, k0=k0, pe0=pe0, pO=pO, kvh=kvh: e.matmul(pO[:, 0:nqa], lhsT=Vg[:, k0, kvh * 128:(kvh + 1) * 128], rhs=pe0[:, 0:nqa],
                                                                                     start=(k0 == 0), stop=(k0 == nkt - 1)), r=[pek0, "Vb"], w=[pOk])
                                T(lambda e, k0=k0, pe0=pe0, pZ=pZ: e.matmul(pZ[:, 0:nqa], lhsT=ones_bf[:], rhs=pe0[:, 0:nqa], start=(k0 == 0), stop=(k0 == nkt - 1)),
                                  r=[pek0], w=[pZk])
                        rs, rsk = rs_r.next()
                        V(lambda e, rs=rs, pZ=pZ: e.reciprocal(out=rs[:, 0:nqa], in_=pZ[:, 0:nqa]), r=[pZk], w=[rsk])
                        V(lambda e, rs=rs, pO=pO, ya=ya, h=h: e.tensor_tensor(out=ya[:, h, 0:nqa], in0=pO[:, 0:nqa], in1=rs[:, 0:nqa], op=ALU.mult), r=[pOk, rsk], w=[yak])
                    S.slot = 10 * (bi + 1) + 9
                    for j, t in enumerate(blk):
                        si = g["sbase"] + g["full"].index(t)
                        DS(lambda e, j=j, si=si, ya=ya: e.dma_start(out=ya_scr[si].rearrange("p (h d) -> p h d", h=8), in_=ya[:, :, j * 128:(j + 1) * 128]), r=[yak], w=["ya_scr"])
                for blk_ in blocks:
                    _blk(blk_)
            for g_ in groups:
                _grp(g_)
            S.emit_phase()

        kvst.close()
        if flags.get("stop_after") == "B1":
            sbst.close()
            return nc
        with contextlib.ExitStack() as st:
            def sb(name, shape, dt):
                return st.enter_context(nc.sbuf_tensor(name, list(shape), dt))

            def ps(name, shape, dt):
                return st.enter_context(nc.psum_tensor(name, list(shape), dt))

            def rot(name, shape, dt, n):
                return Rot([sb("%s%d" % (name, i), shape, dt) for i in range(n)], name)

            slotbase[0] = 0.0
            wB = sb("wB", [128, 8, 3072], BF16)
            hT_r = rot("hTC", [128, 1024], BF16, 3)
            rr_r = rot("rrC", [128, 2, 64], F32, 3)
            tA_r = rot("tAC", [128, 8, 64], F32, 2)
            tB_r = rot("tBC", [128, 8, 64], F32, 2)
            raw_r = rot("rawC", [128, 512], F32, 2)
            rqb_r = rot("rqbC", [128, 8, 64], BF16, 2)
            rkb_r = rot("rkbC", [128, 8, 64], BF16, 2)
            qdf_r = rot("qdfC", [128, 8, 64], BF16, 2)
            qdb_r = rot("qdbC", [128, 8, 64], BF16, 2)
            kdf_r = rot("kdfC", [128, 8, 64], BF16, 4)
            rvb_r = rot("rvbC", [128, 1024], BF16, 4)
            sg_r = rot("sgC", [128, 1024], F32, 5)
            qkT_r = rot("qkTC", [128, 8, 128], BF16, 2)
            qdT_r = rot("qdTC", [128, 8, 128], BF16, 3)
            msk_r = rot("mskC", [128, 1024], BF16, 2)
            junk = sb("junkC", [128, 128], F32)
            ssr_r = rot("ssrC", [128, 8], F32, 2)
            yrb_r = rot("yrbC", [128, 1024], BF16, 2)
            yrT_r = rot("yrTC", [128, 1024], BF16, 2)
            SF = sb("SFC", [128, 4, 128], F32)
            SFb = sb("SFbC", [128, 4, 128], BF16)
            pP_r = Rot([ps("pPC%d" % i, [128, 512], F32) for i in range(3)], "pPC")
            pTa = ps("pTaC", [128, 8, 128], BF16)
            pSc = [ps("pScC%d" % i, [128, 4, 128], F32) for i in range(2)]
            pOr = [ps("pOrC%d" % i, [128, 4, 128], F32) for i in range(2)]
            for i in range(6):
                DG(lambda e, i=i: e.dma_start(out=wB[:, :, i * 512:(i + 1) * 512], in_=w_in[:, :, 1536 + i * 512:1536 + (i + 1) * 512]), w=["wB%d" % i])

            def _grp(g):
                samp, tb = g["samp"], g["tb"]
                fl = g["full"]
                base = slotbase[0]
                slotbase[0] += len(fl) + 10
                S.slot = base - 3
                if samp:
                    s0v = s0f.rearrange("p (a b) d -> p a b d", b=2)
                    DS(lambda e: e.dma_start(out=SF[0:64], in_=s0v[:, :, 0, :]), w=["SF"])
                    DS(lambda e: e.dma_start(out=SF[64:128], in_=s0v[:, :, 1, :]), w=["SF"])
                else:
                    G(lambda e: e.memset(SF[:], 0.0), w=["SF"])
                G(lambda e: e.tensor_copy(out=SFb[:], in_=SF[:]), r=["SF"], w=["SFb"])
                for oi, t in enumerate(fl):
                    slot = g["sbase"] + oi
                    S.slot = base + oi - 2
                    hT, hTk = hT_r.next()
                    DS(lambda e, slot=slot, hT=hT: e.dma_start(out=hT[:], in_=hT_scr[slot]), w=[hTk])
                    rr, rrk = rr_r.next()
                    DS(lambda e, t=t, rr=rr: e.dma_start(out=rr[:], in_=g["rr"][t * 128:(t + 1) * 128]), w=[rrk])
                    S.slot = base + oi
                    rqb, rqbk = rqb_r.next()
                    rkb, rkbk = rkb_r.next()
                    qdf, qdfk = qdf_r.next()
                    qdb, qdbk = qdb_r.next()
                    kdf, kdfk = kdf_r.next()
                    rvb, rvbk = rvb_r.next()
                    sg, sgk = sg_r.next()
                    for cgi in range(6):
                        pp, ppk = pP_r.next()
                        for k in range(8):
                            T(lambda e, k=k, cgi=cgi, pp=pp, hT=hT: e.matmul(pp[:], lhsT=hT[:, k * 128:(k + 1) * 128], rhs=wB[:, k, cgi * 512:(cgi + 1) * 512],
                                                                             start=(k == 0), stop=(k == 7)), r=[hTk, "wB%d" % cgi], w=[ppk])
                        if cgi == 0:
                            tA, tAk = tA_r.next()
                            tB, tBk = tB_r.next()
                            raw, rawk = raw_r.next()
                            A(lambda e, raw=raw, pp=pp: e.copy(out=raw[:], in_=pp[:]), r=[ppk], w=[rawk])
                            rope_ret_ops(V, G, raw[:].rearrange("p (h d) -> p h d", h=8), rawk, rr, rrk, rqb, rqbk, tA, tB, tAk, tBk)
                            S.slot = base + oi + 0.5
                            V(lambda e, tb=tb, qdf=qdf, rqb=rqb: e.tensor_tensor(out=qdf[:], in0=rqb[:], in1=QDF[tb][:].unsqueeze(2).to_broadcast([128, 8, 64]), op=ALU.mult), r=[rqbk], w=[qdfk])
                            V(lambda e, tb=tb, qdb=qdb, rqb=rqb: e.tensor_tensor(out=qdb[:], in0=rqb[:], in1=QDB[tb][:].unsqueeze(2).to_broadcast([128, 8, 64]), op=ALU.mult), r=[rqbk], w=[qdbk])
                            S.slot = base + oi
                        elif cgi == 1:
                            tA, tAk = tA_r.next()
                            tB, tBk = tB_r.next()
                            raw, rawk = raw_r.next()
                            A(lambda e, raw=raw, pp=pp: e.copy(out=raw[:], in_=pp[:]), r=[ppk], w=[rawk])
                            rope_ret_ops(V, G, raw[:].rearrange("p (h d) -> p h d", h=8), rawk, rr, rrk, rkb, rkbk, tA, tB, tAk, tBk)
                            S.slot = base + oi + 0.5
                            V(lambda e, tb=tb, kdf=kdf, rkb=rkb: e.tensor_tensor(out=kdf[:], in0=rkb[:], in1=KDF[tb][:].unsqueeze(2).to_broadcast([128, 8, 64]), op=ALU.mult), r=[rkbk], w=[kdfk])
                            S.slot = base + oi
                        elif cgi in (2, 3):
                            A(lambda e, cgi=cgi, pp=pp, rvb=rvb: e.copy(out=rvb[:, (cgi - 2) * 512:(cgi - 1) * 512], in_=pp[:]), r=[ppk], w=[rvbk + "_%d" % (cgi - 2)])
                        else:
                            A(lambda e, cgi=cgi, pp=pp, sg=sg: e.activation(out=sg[:, (cgi - 4) * 512:(cgi - 3) * 512], in_=pp[:], func=AF.Silu), r=[ppk], w=[sgk + "_%d" % (cgi - 4)])
                    rvbks = [rvbk + "_0", rvbk + "_1"]
                    S.slot = base + oi + 1
                    qkT, qkTk = qkT_r.next()
                    qdT, qdTk = qdT_r.next()
                    for hp in range(4):
                        T(lambda e, hp=hp, rqb=rqb: e.transpose(out=pTa[:, hp, :], in_=rqb[:, 2 * hp:2 * hp + 2, :].rearrange("p a d -> p (a d)"), identity=ident[:]), r=[rqbk], w=["pTaC"])
                        T(lambda e, hp=hp, rkb=rkb: e.transpose(out=pTa[:, 4 + hp, :], in_=rkb[:, 2 * hp:2 * hp + 2, :].rearrange("p a d -> p (a d)"), identity=ident[:]), r=[rkbk], w=["pTaC"])
                    A(lambda e, qkT=qkT: e.copy(out=qkT[:], in_=pTa[:]), r=["pTaC"], w=[qkTk])
                    for hp in range(4):
                        T(lambda e, hp=hp, qdf=qdf: e.transpose(out=pTa[:, hp, :], in_=qdf[:, 2 * hp:2 * hp + 2, :].rearrange("p a d -> p (a d)"), identity=ident[:]), r=[qdfk], w=["pTaC"])
                        T(lambda e, hp=hp, qdb=qdb: e.transpose(out=pTa[:, 4 + hp, :], in_=qdb[:, 2 * hp:2 * hp + 2, :].rearrange("p a d -> p (a d)"), identity=ident[:]), r=[qdbk], w=["pTaC"])
                    V(lambda e, qdT=qdT: e.tensor_copy(out=qdT[:], in_=pTa[:]), r=["pTaC"], w=[qdTk])
                    S.slot = base + oi + 2
                    msk, mskk = msk_r.next()
                    for h in range(8):
                        lo = 64 * (h % 2)
                        T(lambda e, h=h, lo=lo, qkT=qkT: e.matmul(pSc[h % 2][:, h // 2, :], lhsT=qkT[lo:lo + 64, 4 + h // 2, :], rhs=qkT[lo:lo + 64, h // 2, :], start=True, stop=True),
                          r=[qkTk], w=["pScC%d" % (h % 2)])
                    mskv = msk[:].rearrange("p (a b d) -> p a b d", a=4, b=2)
                    for par in range(2):
                        V(lambda e, par=par, tb=tb, mskv=mskv: e.tensor_tensor(out=mskv[:, :, par, :], in0=pSc[par][:],
                                                                                in1=MT[tb][:].rearrange("p (a b d) -> p a b d", a=4, b=2)[:, :, par, :], op=ALU.mult),
                          r=["pScC%d" % par], w=[mskk + "_%d" % par])
                    S.slot = base + oi + 3
                    ssr, ssrk = ssr_r.next()
                    last = (oi == len(fl) - 1)
                    do_upd = not (samp and last)
                    if do_upd:
                        for h in range(8):
                            T(lambda e, h=h, kdf=kdf, rvb=rvb: e.matmul(pSc[h // 4][:, h % 4, :], lhsT=kdf[:, 2 * (h // 2):2 * (h // 2) + 2, :].rearrange("p a d -> p (a d)"),
                                                                        rhs=rvb[:, h * 128:(h + 1) * 128], start=True, stop=True), r=[kdfk] + rvbks, w=["pScC%d" % (h // 4)])
                    for h in range(8):
                        lo = 64 * (h % 2)
                        T(lambda e, h=h, msk=msk, rvb=rvb: e.matmul(pOr[h % 2][:, h // 2, :], lhsT=msk[:, h * 128:(h + 1) * 128], rhs=rvb[:, h * 128:(h + 1) * 128], start=True, stop=False),
                          r=[mskk + "_%d" % (h % 2)] + rvbks, w=["pOrC%d" % (h % 2)])
                        T(lambda e, h=h, lo=lo, qdT=qdT: e.matmul(pOr[h % 2][:, h // 2, :], lhsT=qdT[lo:lo + 64, h // 2, :], rhs=SFb[lo:lo + 64, h // 2, :], start=False, stop=False),
                          r=[qdTk, "SFb"], w=["pOrC%d" % (h % 2)])
                        T(lambda e, h=h, lo=lo, slot=slot, qdT=qdT: e.matmul(pOr[h % 2][:, h // 2, :], lhsT=qdT[lo:lo + 64, 4 + h // 2, :],
                                                                             rhs=SBs[lo:lo + 64, slot, (h // 2) * 128:(h // 2 + 1) * 128], start=False, stop=True),
                          r=[qdTk], w=["pOrC%d" % (h % 2)])
                    if do_upd:
                        cdv = CDF[tb][:].rearrange("p (a b) -> p a b", b=2)
                        for par in range(2):
                            lo = 64 * par
                            V(lambda e, lo=lo, par=par, cdv=cdv: e.tensor_tensor(out=SF[lo:lo + 64], in0=SF[lo:lo + 64],
                                                                                  in1=cdv[lo:lo + 64, :, par].unsqueeze(2).to_broadcast([64, 4, 128]), op=ALU.mult), r=["SF"], w=["SF"])
                            for bk in range(2):
                                V(lambda e, lo=lo, par=par, bk=bk: e.tensor_tensor(out=SF[lo:lo + 64, 2 * bk:2 * bk + 2, :], in0=SF[lo:lo + 64, 2 * bk:2 * bk + 2, :],
                                                                                    in1=pSc[bk][lo:lo + 64].rearrange("p (a b) d -> p a b d", b=2)[:, :, par, :], op=ALU.add),
                                  r=["SF", "pScC%d" % bk], w=["SF"])
                        V(lambda e: e.tensor_copy(out=SFb[:], in_=SF[:]), r=["SF"], w=["SFb"])
                        if (not samp) and last:
                            nf_ = nsf[g["p"]].rearrange("p (a b) d -> p a b d", b=2)
                            for par in range(2):
                                DS(lambda e, par=par, nf_=nf_: e.dma_start(out=nf_[:, :, par, :], in_=SF[64 * par:64 * par + 64]), r=["SF"], w=["nsf"])
                    for h in range(8):
                        A(lambda e, h=h, ssr=ssr: e.activation(out=junk[:], in_=pOr[h % 2][:, h // 2, :], func=AF.Square, scale=float(128 ** -0.5), accum_out=ssr[:, h:h + 1]),
                          r=["pOrC%d" % (h % 2)], w=[ssrk])
                    A(lambda e, ssr=ssr: e.activation(out=ssr[:], in_=ssr[:], func=AF.Sqrt, bias=EPS), r=[ssrk], w=[ssrk])
                    S.slot = base + oi + 4
                    yrb, yrbk = yrb_r.next()
                    V(lambda e, ssr=ssr: e.reciprocal(out=ssr[:], in_=ssr[:]), r=[ssrk], w=[ssrk])
                    for h in range(8):
                        V(lambda e, h=h, ssr=ssr, yrb=yrb, sg=sg: e.scalar_tensor_tensor(out=yrb[:, h * 128:(h + 1) * 128], in0=pOr[h % 2][:, h // 2, :], scalar=ssr[:, h:h + 1],
                                                                                         in1=sg[:, h * 128:(h + 1) * 128], op0=ALU.mult, op1=ALU.mult),
                          r=["pOrC%d" % (h % 2), ssrk, sgk + "_0", sgk + "_1"], w=[yrbk])
                    S.slot = base + oi + 5
                    for k in range(8):
                        T(lambda e, k=k, yrb=yrb: e.transpose(out=pTa[:, k, :], in_=yrb[:, k * 128:(k + 1) * 128], identity=ident[:]), r=[yrbk], w=["pTaC"])
                    yrT, yrTk = yrT_r.next()
                    A(lambda e, yrT=yrT: e.copy(out=yrT[:], in_=pTa[:].rearrange("p k d -> p (k d)")), r=["pTaC"], w=[yrTk])
                    DS(lambda e, yrT=yrT, slot=slot: e.dma_start(out=yr_scr[slot], in_=yrT[:]), r=[yrTk], w=["yr_scr"])
            for g_ in groups:
                _grp(g_)
            S.emit_phase()

        sbst.close()
        if flags.get("stop_after") == "B2":
            return nc

        with contextlib.ExitStack() as st:
            def sb(name, shape, dt):
                return st.enter_context(nc.sbuf_tensor(name, list(shape), dt))

            def ps(name, shape, dt):
                return st.enter_context(nc.psum_tensor(name, list(shape), dt))

            def rot(name, shape, dt, n):
                return Rot([sb("%s%d" % (name, i), shape, dt) for i in range(n)], name)

            slotbase[0] = 0.0
            wg = sb("wgD", [128, 8, 2048], BF16)
            wao = sb("waoD", [128, 8, 1024], BF16)
            wro = sb("wroD", [128, 8, 1024], BF16)
            wo = sb("woD", [128, 8, 1024], BF16)
            g1b = [sb("g1bD%d" % j, [128, D], F32) for j in range(2)]
            G2b = [sb("G2bD%d" % j, [128, D], F32) for j in range(2)]
            sh2b = [sb("sh2bD%d" % j, [128, D], F32) for j in range(2)]
            hT_r = rot("hTD", [128, 1024], BF16, 3)
            ya_r = rot("yaD", [128, 1024], BF16, 4)
            yr_r = rot("yrD", [128, 1024], BF16, 4)
            x_r = rot("xD", [128, D], F32, 3)
            gat_r = rot("gatD", [128, D], BF16, 4)
            Y_r = rot("YD", [128, D], F32, 2)
            Tt_r = rot("TtD", [128, D], F32, 2)
            To_r = rot("ToD", [128, D], F32, 1)
            yb_r = rot("ybD", [128, D], BF16, 2)
            yT_r = rot("yTD", [128, 8, 128], BF16, 2)
            xn_r = rot("xnD", [128, D], F32, 2)
            junk = sb("junkD", [128, D], BF16)
            ss2_r = rot("ss2D", [128, 1], F32, 2)
            tmpf_r = rot("tmpfD", [128, D], F32, 2)
            h2b_r = rot("h2bD", [128, D], BF16, 2)
            h2T_r = rot("h2TD", [128, 1024], BF16, 2)
            pP_r = Rot([ps("pPD%d" % i, [128, 512], F32) for i in range(6)], "pPD")
            pT1 = ps("pT1D", [128, 8, 128], BF16)
            pT2 = ps("pT2D", [128, 8, 128], BF16)
            for i in range(4):
                DG(lambda e, i=i: e.dma_start(out=wg[:, :, i * 512:(i + 1) * 512], in_=w_in[:, :, 4608 + i * 512:4608 + (i + 1) * 512]), w=["wg%d" % i])
            for (wt, src, nm) in ((wao, w_ao, "wao"), (wro, w_ro, "wro"), (wo, w_o, "wo")):
                for i in range(2):
                    DG(lambda e, i=i, wt=wt, src=src: e.dma_start(out=wt[:, :, i * 512:(i + 1) * 512], in_=src[:, :, i * 512:(i + 1) * 512]), w=["%s%d" % (nm, i)])
            for j in range(2):
                DS(lambda e, j=j: e.dma_start(out=g1b[j][:], in_=mod_scr[j, 2]), w=["g1b"])
                DS(lambda e, j=j: e.dma_start(out=G2b[j][:], in_=mod_scr[j, 4]), w=["G2b"])
                DS(lambda e, j=j: e.dma_start(out=sh2b[j][:], in_=mod_scr[j, 3]), w=["sh2b"])

            def _grp(g):
                cond = g["cond"]
                fl = g["full"]
                base = slotbase[0]
                slotbase[0] += len(fl) + 10
                for oi, t in enumerate(fl):
                    S.slot = base + oi - 2
                    si = g["sbase"] + oi
                    hT, hTk = hT_r.next()
                    DS(lambda e, si=si, hT=hT: e.dma_start(out=hT[:], in_=hT_scr[si]), w=[hTk])
                    ya, yak = ya_r.next()
                    DS(lambda e, si=si, ya=ya: e.dma_start(out=ya[:], in_=ya_scr[si]), w=[yak])
                    yr, yrk = yr_r.next()
                    DS(lambda e, si=si, yr=yr: e.dma_start(out=yr[:], in_=yr_scr[si]), w=[yrk])
                    S.slot = base + oi + 1
                    xt, xk = x_r.next()
                    DS(lambda e, t=t, xt=xt: e.dma_start(out=xt[:], in_=g["x"][t * 128:(t + 1) * 128, :]), w=[xk])
                    slot = si
                    S.slot = base + oi
                    Y, Yk = Y_r.next()
                    Tt, Ttk = Tt_r.next()
                    yb, ybk = yb_r.next()
                    gats = []
                    for br in range(2):
                        gat, gatk = gat_r.next()
                        gats.append((gat, gatk))
                        for hf in range(2):
                            pp, ppk = pP_r.next()
                            for k in range(8):
                                T(lambda e, k=k, pp=pp, hT=hT, br=br, hf=hf: e.matmul(pp[:], lhsT=hT[:, k * 128:(k + 1) * 128],
                                                                                      rhs=wg[:, k, br * 1024 + hf * 512:br * 1024 + (hf + 1) * 512], start=(k == 0), stop=(k == 7)),
                                  r=[hTk, "wg%d" % (2 * br + hf)], w=[ppk])
                            A(lambda e, pp=pp, hf=hf, gat=gat: e.activation(out=gat[:, hf * 512:(hf + 1) * 512], in_=pp[:], func=AF.Sigmoid), r=[ppk], w=[gatk + "_%d" % hf])
                    S.slot = base + oi + 1
                    for br in range(2):
                        src, srck = (ya, yak) if br == 0 else (yr, yrk)
                        wt, wn = (wao, "wao") if br == 0 else (wro, "wro")
                        gat, gatk = gats[br]
                        for hf in range(2):
                            pp, ppk = pP_r.next()
                            for k in range(8):
                                T(lambda e, k=k, pp=pp, src=src, wt=wt, hf=hf: e.matmul(pp[:], lhsT=src[:, k * 128:(k + 1) * 128], rhs=wt[:, k, hf * 512:(hf + 1) * 512],
                                                                                        start=(k == 0), stop=(k == 7)), r=[srck, "%s%d" % (wn, hf)], w=[ppk])
                            if br == 0:
                                V(lambda e, pp=pp, hf=hf, gat=gat, Y=Y: e.tensor_tensor(out=Y[:, hf * 512:(hf + 1) * 512], in0=pp[:], in1=gat[:, hf * 512:(hf + 1) * 512], op=ALU.mult),
                                  r=[ppk, gatk + "_%d" % hf], w=[Yk + "_%d" % hf])
                            else:
                                V(lambda e, pp=pp, hf=hf, gat=gat, Tt=Tt: e.tensor_tensor(out=Tt[:, hf * 512:(hf + 1) * 512], in0=pp[:], in1=gat[:, hf * 512:(hf + 1) * 512], op=ALU.mult),
                                  r=[ppk, gatk + "_%d" % hf], w=[Ttk + "_%d" % hf])
                                G(lambda e, hf=hf, Y=Y, Tt=Tt, yb=yb: e.tensor_tensor(out=yb[:, hf * 512:(hf + 1) * 512], in0=Y[:, hf * 512:(hf + 1) * 512], in1=Tt[:, hf * 512:(hf + 1) * 512], op=ALU.add),
                                  r=[Yk + "_%d" % hf, Ttk + "_%d" % hf], w=[ybk + "_%d" % hf])
                    S.slot = base + oi + 2
                    for k in range(8):
                        T(lambda e, k=k, yb=yb: e.transpose(out=pT1[:, k, :], in_=yb[:, k * 128:(k + 1) * 128], identity=ident[:]), r=[ybk + "_%d" % (k // 4)], w=["pT1D"])
                    yT, yTk = yT_r.next()
                    A(lambda e, yT=yT: e.copy(out=yT[:], in_=pT1[:]), r=["pT1D"], w=[yTk])
                    S.slot = base + oi + 3
                    xn, xnk = xn_r.next()
                    To, Tok = To_r.next()
                    for hf in range(2):
                        pp, ppk = pP_r.next()
                        for k in range(8):
                            T(lambda e, k=k, pp=pp, hf=hf, yT=yT: e.matmul(pp[:], lhsT=yT[:, k, :], rhs=wo[:, k, hf * 512:(hf + 1) * 512], start=(k == 0), stop=(k == 7)),
                              r=[yTk, "wo%d" % hf], w=[ppk])
                        V(lambda e, pp=pp, hf=hf, cond=cond, To=To: e.tensor_tensor(out=To[:, hf * 512:(hf + 1) * 512], in0=pp[:], in1=g1b[cond][:, hf * 512:(hf + 1) * 512], op=ALU.mult),
                          r=[ppk, "g1b"], w=[Tok + "_%d" % hf])
                        G(lambda e, hf=hf, xn=xn, xt=xt, To=To: e.tensor_tensor(out=xn[:, hf * 512:(hf + 1) * 512], in0=To[:, hf * 512:(hf + 1) * 512], in1=xt[:, hf * 512:(hf + 1) * 512], op=ALU.add),
                          r=[Tok + "_%d" % hf, xk], w=[xnk + "_%d" % hf])
                    DS(lambda e, xn=xn, slot=slot: e.dma_start(out=xs_scr[slot], in_=xn[:]), r=[xnk + "_0", xnk + "_1"], w=["xs_scr"], sk=xnk)
                    S.slot = base + oi + 4
                    ss2, ss2k = ss2_r.next()
                    tmpf, tmpfk = tmpf_r.next()
                    h2b, h2bk = h2b_r.next()
                    A(lambda e, xn=xn, ss2=ss2: e.activation(out=junk[:], in_=xn[:], func=AF.Square, scale=float(D ** -0.5), accum_out=ss2[:]), r=[xnk + "_0", xnk + "_1"], w=[ss2k])
                    A(lambda e, ss2=ss2: e.activation(out=ss2[:], in_=ss2[:], func=AF.Sqrt, bias=EPS), r=[ss2k], w=[ss2k])
                    V(lambda e, ss2=ss2: e.reciprocal(out=ss2[:], in_=ss2[:]), r=[ss2k], w=[ss2k])
                    S.slot = base + oi + 4.5
                    V(lambda e, xn=xn, cond=cond, ss2=ss2, tmpf=tmpf: e.scalar_tensor_tensor(out=tmpf[:], in0=xn[:], scalar=ss2[:, 0:1], in1=G2b[cond][:], op0=ALU.mult, op1=ALU.mult),
                      r=[xnk + "_0", xnk + "_1", ss2k, "G2b"], w=[tmpfk])
                    G(lambda e, cond=cond, tmpf=tmpf, h2b=h2b: e.tensor_tensor(out=h2b[:], in0=tmpf[:], in1=sh2b[cond][:], op=ALU.add), r=[tmpfk, "sh2b"], w=[h2bk])
                    S.slot = base + oi + 5
                    for k in range(8):
                        T(lambda e, k=k, h2b=h2b: e.transpose(out=pT2[:, k, :], in_=h2b[:, k * 128:(k + 1) * 128], identity=ident[:]), r=[h2bk], w=["pT2D"])
                    h2T, h2Tk = h2T_r.next()
                    A(lambda e, h2T=h2T: e.copy(out=h2T[:], in_=pT2[:].rearrange("p k d -> p (k d)")), r=["pT2D"], w=[h2Tk])
                    DS(lambda e, h2T=h2T, slot=slot: e.dma_start(out=h2_scr[slot], in_=h2T[:]), r=[h2Tk], w=["h2_scr"])
            for g_ in groups:
                _grp(g_)
            S.emit_phase()

        if flags.get("stop_after") == "B3":
            return nc
        with contextlib.ExitStack() as st:
            def sb(name, shape, dt):
                return st.enter_context(nc.sbuf_tensor(name, list(shape), dt))

            def ps(name, shape, dt):
                return st.enter_context(nc.psum_tensor(name, list(shape), dt))

            wu = sb("wuE", [128, 8, 5632], BF16)
            slotbase[0] = 0.0
            wd = sb("wdE", [128, 22, 1024], BF16)
            g2b = [sb("g2bE%d" % j, [128, D], F32) for j in range(2)]
            fgb = sb("fgbE", [128, D], F32)
            cvt_ = [sb("cvE%d" % j, [128, 4, 44], F32) for j in range(2)]
            h2_r = Rot([sb("h2E%d" % i, [128, 8, 386], BF16) for i in range(2)], "h2E")
            act_r = Rot([sb("actE%d" % i, [128, 22, 384], BF16) for i in range(1)], "actE")
            t1_r = Rot([sb("t1E%d" % i, [128, 384], F32) for i in range(6)], "t1E")
            sgl_r = Rot([sb("sglE%d" % i, [128, 384], F32) for i in range(2)], "sglE")
            xn_r = Rot([sb("xnE%d" % i, [128, D], F32) for i in range(1)], "xnE")
            xf_r = Rot([sb("xfE%d" % i, [128, D], F32) for i in range(1)], "xfE")
            junk = sb("junkE", [128, D], F32)
            ssf = sb("ssfE", [128, 1], F32)
            yo_r = Rot([sb("yoE%d" % i, [128, D], F32) for i in range(1)], "yoE")
            pU_r = Rot([ps("pUE%d" % i, [128, 512], F32) for i in range(6)], "pUE")
            pD_r = Rot([ps("pDE%d" % i, [128, 512], F32) for i in range(2)], "pDE")
            for i in (0, 5, 1, 6, 2, 7, 3, 8, 4, 9, 10):
                DG(lambda e, i=i: e.dma_start(out=wu[:, :, i * 512:(i + 1) * 512], in_=w_up[:, :, i * 512:(i + 1) * 512]), w=["wu%d" % i])
            for i in range(11):
                DG(lambda e, i=i: e.dma_start(out=wd[:, 2 * i:2 * i + 2, :], in_=w_dn[:, 2 * i:2 * i + 2, :]), w=["wd%d" % i])
            for j in range(2):
                DS(lambda e, j=j: e.dma_start(out=g2b[j][:], in_=mod_scr[j, 5]), w=["g2b"])
            DS(lambda e: e.dma_start(out=fgb[:], in_=fg.partition_broadcast(128)), w=["fgb"])
            DS(lambda e: e.dma_start(out=cvt_[0][:], in_=convs), w=["cv"])
            DS(lambda e: e.dma_start(out=cvt_[1][:], in_=convp), w=["cv"])
            def _grp(g):
                cond = g["cond"]
                cvw = cvt_[g["tb"]]
                own = g["own"]
                mblocks = [list(range(i, min(i + 3, own))) for i in range(0, own, 3)]
                def _blk(blk):
                    n = 128 * len(blk)
                    bb = slotbase[0]
                    slotbase[0] += 40
                    S.slot = bb - 20
                    h2, h2k = h2_r.next()
                    wkeys = []
                    t0 = blk[0]
                    if t0 == 0:
                        G(lambda e, h2=h2: e.memset(h2[:, :, 0:1], 0.0), w=[h2k + "_L"])
                    else:
                        DS(lambda e, h2=h2, t0=t0: e.dma_start(out=h2[:, :, 0:1], in_=h2_scr[g["sbase"] + t0 - 1].rearrange("p (k d) -> p k d", k=8)[:, :, 127:128], allow_slow_non_contiguous=True), w=[h2k + "_L"])
                    for j, t in enumerate(blk):
                        DS(lambda e, h2=h2, j=j, t=t: e.dma_start(out=h2[:, :, 1 + j * 128:1 + (j + 1) * 128], in_=h2_scr[g["sbase"] + t].rearrange("p (k d) -> p k d", k=8)),
                           w=[h2k + "_%d" % j])
                    t1 = blk[-1] + 1
                    if t1 < len(g["full"]):
                        DS(lambda e, h2=h2, t1=t1, n=n: e.dma_start(out=h2[:, :, n + 1:n + 2], in_=h2_scr[g["sbase"] + t1].rearrange("p (k d) -> p k d", k=8)[:, :, 0:1], allow_slow_non_contiguous=True), w=[h2k + "_R"])
                    else:
                        G(lambda e, h2=h2, n=n: e.memset(h2[:, :, n + 1:n + 2], 0.0), w=[h2k + "_R"])
                    h2keys = [h2k + "_L", h2k + "_R"] + [h2k + "_%d" % j for j in range(len(blk))]
                    act, actk = act_r.next()
                    for c in range(22):
                        res = []
                        for which in range(2):
                            ch = c + 22 * which
                            S.slot = bb + c
                            pu, puk = pU_r.next()
                            for k in range(8):
                                T(lambda e, k=k, pu=pu, ch=ch, h2=h2, n=n: e.matmul(pu[:, 0:n + 2], lhsT=wu[:, k, ch * 128:(ch + 1) * 128], rhs=h2[:, k, 0:n + 2],
                                                                                    start=(k == 0), stop=(k == 7)), r=h2keys + ["wu%d" % (ch // 4)], w=[puk])
                            S.slot = bb + c + 1
                            t1b, t1k = t1_r.next()
                            A(lambda e, pu=pu, ch=ch, t1b=t1b, n=n, cvw=cvw: e.activation(out=t1b[:, 0:n], in_=pu[:, 1:n + 1], func=AF.Identity, scale=cvw[:, 1, ch:ch + 1], bias=cvw[:, 3, ch:ch + 1]),
                              r=[puk, "cv"], w=[t1k])
                            V(lambda e, pu=pu, ch=ch, t1b=t1b, n=n, cvw=cvw: e.scalar_tensor_tensor(out=t1b[:, 0:n], in0=pu[:, 0:n], scalar=cvw[:, 0, ch:ch + 1], in1=t1b[:, 0:n],
                                                                                                    op0=ALU.mult, op1=ALU.add), r=[puk, t1k, "cv"], w=[t1k])
                            S.slot = bb + c + 2
                            V(lambda e, pu=pu, ch=ch, t1b=t1b, n=n, cvw=cvw: e.scalar_tensor_tensor(out=t1b[:, 0:n], in0=pu[:, 2:n + 2], scalar=cvw[:, 2, ch:ch + 1], in1=t1b[:, 0:n],
                                                                                                    op0=ALU.mult, op1=ALU.add), r=[puk, t1k, "cv"], w=[t1k])
                            res.append((t1b, t1k))
                        (ta, tak), (tg, tgk) = res
                        S.slot = bb + c + 3
                        sgl, sglk = sgl_r.next()
                        A(lambda e, tg=tg, sgl=sgl, n=n: e.activation(out=sgl[:, 0:n], in_=tg[:, 0:n], func=AF.Silu), r=[tgk], w=[sglk])
                        G(lambda e, ta=ta, sgl=sgl, act=act, c=c, n=n: e.tensor_tensor(out=act[:, c, 0:n], in0=ta[:, 0:n], in1=sgl[:, 0:n], op=ALU.mult), r=[tak, sglk], w=[actk + "_%d" % c])
                    actkeys = [actk + "_%d" % c for c in range(22)]
                    S.slot = bb + 26
                    for j, t in enumerate(blk):
                        slot = g["sbase"] + t
                        xn, xnk = xn_r.next()
                        DS(lambda e, xn=xn, slot=slot: e.dma_start(out=xn[:], in_=xs_scr[slot]), w=[xnk])
                        xf, xfk = xf_r.next()
                        for hf in range(2):
                            pd, pdk = pD_r.next()
                            for c in range(22):
                                T(lambda e, c=c, pd=pd, act=act, j=j, hf=hf: e.matmul(pd[:], lhsT=act[:, c, j * 128:(j + 1) * 128], rhs=wd[:, c, hf * 512:(hf + 1) * 512],
                                                                                      start=(c == 0), stop=(c == 21)), r=actkeys + ["wd%d" % (c // 2)], w=[pdk])
                            V(lambda e, pd=pd, hf=hf, xf=xf, cond=cond: e.tensor_tensor(out=xf[:, hf * 512:(hf + 1) * 512], in0=pd[:], in1=g2b[cond][:, hf * 512:(hf + 1) * 512], op=ALU.mult),
                              r=[pdk, "g2b"], w=[xfk + "_%d" % hf])
                            G(lambda e, hf=hf, xf=xf, xn=xn: e.tensor_tensor(out=xf[:, hf * 512:(hf + 1) * 512], in0=xf[:, hf * 512:(hf + 1) * 512], in1=xn[:, hf * 512:(hf + 1) * 512], op=ALU.add),
                              r=[xfk + "_%d" % hf, xnk], w=[xfk + "_%d" % hf])
                        xfkeys = [xfk + "_0", xfk + "_1"]
                        A(lambda e, xf=xf: e.activation(out=junk[:], in_=xf[:], func=AF.Square, accum_out=ssf[:]), r=xfkeys, w=["junkE", "ssfE"])
                        V(lambda e: e.tensor_scalar(out=ssf[:], in0=ssf[:], scalar1=1.0 / D, scalar2=EPS, op0=ALU.mult, op1=ALU.add), r=["ssfE"], w=["ssfE"])
                        A(lambda e: e.activation(out=ssf[:], in_=ssf[:], func=AF.Sqrt), r=["ssfE"], w=["ssfE"])
                        V(lambda e: e.reciprocal(out=ssf[:], in_=ssf[:]), r=["ssfE"], w=["ssfE"])
                        yo, yok = yo_r.next()
                        V(lambda e, xf=xf, yo=yo: e.scalar_tensor_tensor(out=yo[:], in0=xf[:], scalar=ssf[:, 0:1], in1=fgb[:], op0=ALU.mult, op1=ALU.mult),
                          r=xfkeys + ["ssfE", "fgb"], w=[yok])
                        if g["samp"]:
                            DS(lambda e, yo=yo, t=t: e.dma_start(out=ys[t * 128:(t + 1) * 128, :], in_=yo[:]), r=[yok], w=["ys"])
                        else:
                            DS(lambda e, yo=yo, t=t, g=g: e.dma_start(out=yp[g["p"] * 256 + t * 128:g["p"] * 256 + (t + 1) * 128, :], in_=yo[:]), r=[yok], w=["yp"])
                for blk_ in mblocks:
                    _blk(blk_)
            for g_ in groups:
                _grp(g_)
            S.emit_phase()
    return nc


def rope_att_ops(V, G, src, srck, tab, tabk, out, outk, H, tA, tB, tAk, tBk):
    tAv = tA[:].rearrange("p a b -> p (a b)")[:, 0:H * 128].rearrange("p (h d) -> p h d", h=H)
    tBv = tB[:].rearrange("p a b -> p (a b)")[:, 0:H * 128].rearrange("p (h d) -> p h d", h=H)
    V(lambda e: e.tensor_tensor(out=tAv, in0=src[:], in1=tab[:, 0, :].unsqueeze(1).to_broadcast([128, H, 128]), op=ALU.mult), r=[srck, tabk], w=[tAk])
    s5 = src[:].rearrange("p h (a b c) -> p h a b c", a=2, b=2)
    o5 = tBv.rearrange("p h (a b c) -> p h a b c", a=2, b=2)
    sn = tab[:, 1, :].rearrange("p (a b c) -> p a b c", a=2, b=2)
    for b in range(2):
        V(lambda e, b=b: e.tensor_tensor(out=o5[:, :, :, b, :], in0=s5[:, :, :, 1 - b, :], in1=sn[:, :, b, :].unsqueeze(1).to_broadcast([128, H, 2, 32]), op=ALU.mult),
          r=[srck, tabk], w=[tBk])
    G(lambda e: e.tensor_tensor(out=out[:], in0=tAv, in1=tBv, op=ALU.add), r=[tAk, tBk], w=[outk])


def rope_ret_ops(V, G, src3, srck, tab, tabk, out, outk, tA, tB, tAk, tBk):
    V(lambda e: e.tensor_tensor(out=tA[:], in0=src3, in1=tab[:, 0, :].unsqueeze(1).to_broadcast([128, 8, 64]), op=ALU.mult), r=[srck, tabk], w=[tAk])
    s4 = src3.rearrange("p h (b c) -> p h b c", b=2)
    o4 = tB[:].rearrange("p h (b c) -> p h b c", b=2)
    sn = tab[:, 1, :].rearrange("p (b c) -> p b c", b=2)
    for b in range(2):
        V(lambda e, b=b: e.tensor_tensor(out=o4[:, :, b, :], in0=s4[:, :, 1 - b, :], in1=sn[:, b, :].unsqueeze(1).to_broadcast([128, 8, 32]), op=ALU.mult),
          r=[srck, tabk], w=[tBk])
    G(lambda e: e.tensor_tensor(out=out[:], in0=tA[:], in1=tB[:], op=ALU.add), r=[tAk, tBk], w=[outk])


def _kc(w, kc):
    n = w.shape[1]
    return np.ascontiguousarray(w.reshape(kc, 128, n).transpose(1, 0, 2))


def _consts():
    inv = (10000.0 ** (-np.arange(32, dtype=np.float32) / np.float32(32))).astype(np.float32)
    j = np.arange(128)[:, None]
    i = np.arange(128)[None, :]
    tabs = np.stack([np.maximum(i - j, 0), (i >= j), np.maximum(j - i, 0), (j >= i)], axis=1).astype(np.float32)
    p = np.arange(128, dtype=np.float32)
    tabe = np.stack([p + 1, 128 - p, 127 - p, p], axis=1).astype(np.float32)
    return inv, np.ascontiguousarray(tabs), np.ascontiguousarray(tabe)


def _rope_tab(pos, inv):
    ang = pos.astype(np.float32)[:, None] * inv[None, :]
    c, s = np.cos(ang).astype(np.float32), np.sin(ang).astype(np.float32)
    return np.concatenate([c, c], 1), np.concatenate([-s, s], 1)


def prep_inputs(inp):
    f = lambda k: np.asarray(inp[k], dtype=np.float32)
    inv, tabs, tabe = _consts()
    shared = {
        "w_in": _kc(f("w_in")[0], 8), "w_mod": _kc(f("w_mod")[0], 8), "b_mod": f("b_mod").reshape(1, 6144),
        "w_ao": _kc(f("w_att_o")[0], 8), "w_ro": _kc(f("w_ret_o")[0], 8), "w_o": _kc(f("w_out")[0], 8),
        "w_up": _kc(f("w_up")[0], 8), "w_dn": _kc(f("w_down")[0], 22),
        "n1g": f("norm1_g").reshape(1, D), "n2g": f("norm2_g").reshape(1, D), "fg": f("final_g").reshape(1, D),
        "qg": f("q_norm_g").reshape(1, 128), "kg": f("k_norm_g").reshape(1, 128),
        "tabs": tabs, "tabe": tabe,
    }
    cw, cb = f("conv_w")[0], f("conv_b")[0]

    def convtab(rev):
        taps = [cw[2 - k] if rev else cw[k] for k in range(3)] + [cb]
        a = np.stack(taps, 0).reshape(4, 44, 128).transpose(2, 0, 1)
        return np.ascontiguousarray(a)

    cr, sr = _rope_tab(np.arange(256), inv)
    shared["rope_rp"] = np.ascontiguousarray(np.stack([cr, sr], 1))
    shared["convp"] = convtab(False)
    xsamp, xprm = f("x_sample"), f("x_prompt")
    cc, cctx = f("c"), f("c_ctx")
    ck, cv = f("cache_k"), f("cache_v")
    sf, sbw = f("state_ret_fwd"), f("state_ret_bwd")
    df, db = f("decay_fwd")[0], f("decay_bwd")[0]
    maps = []
    for c in range(8):
        b, rev = c // 2, c % 2
        t = np.arange(4096)
        if rev:
            t = t[::-1]
        m = dict(shared)
        m["xs"] = np.ascontiguousarray(xsamp[b][t])
        m["xp"] = np.ascontiguousarray(xprm[2 * c:2 * c + 2].reshape(512, D))
        m["cT"] = np.ascontiguousarray(np.stack([cc[b], cctx], 0).reshape(2, 8, 128).transpose(2, 1, 0))
        m["ck"] = np.ascontiguousarray(ck[b, 0])
        m["cv"] = np.ascontiguousarray(cv[b, 0])
        a0, b0 = (sbw, sf) if rev else (sf, sbw)
        m["s0f"] = np.ascontiguousarray(a0[b, 0].transpose(1, 0, 2))
        m["s0b"] = np.ascontiguousarray(b0[b, 0].transpose(1, 0, 2))
        m["dec"] = np.concatenate([db, df, df, db] if rev else [df, db, df, db]).reshape(1, 32).astype(np.float32)
        m["convs"] = convtab(bool(rev))
        c1, s1 = _rope_tab(t // 64, inv)
        c2, s2 = _rope_tab(t % 64, inv)
        m["rope_att"] = np.ascontiguousarray(np.stack([np.concatenate([c1, c2], 1), np.concatenate([s1, s2], 1)], 1))
        c3, s3 = _rope_tab(512 + t, inv)
        m["rope_rs"] = np.ascontiguousarray(np.stack([c3, s3], 1))
        maps.append(m)
    return maps


_NC_CACHE = {}


def run_device(inp, flags=None):
    flags = flags or {}
    key = tuple(sorted(flags.items()))
    if key not in _NC_CACHE:
        _NC_CACHE[key] = build_nc(flags)
    nc = _NC_CACHE[key]
    maps = prep_inputs(inp)
    res = run_bass_kernel_spmd(nc, maps, core_ids=list(range(8)))
    return res.results


def assemble(results):
    y_prompt = np.zeros((16, 256, D), np.float32)
    y_sample = np.zeros((4, 4096, D), np.float32)
    nck = np.zeros((16, 1, 2, 256, 128), np.float32)
    ncv = np.zeros((16, 1, 2, 256, 128), np.float32)
    nf = np.zeros((16, 1, 8, 64, 128), np.float32)
    nb = np.zeros((16, 1, 8, 64, 128), np.float32)
    for c in range(8):
        r = results[c]
        b, rev = c // 2, c % 2
        if rev:
            y_sample[b, 2048:4096] = r["ys"][::-1]
        else:
            y_sample[b, 0:2048] = r["ys"]
        y_prompt[2 * c:2 * c + 2] = r["yp"].reshape(2, 256, D)
        nck[2 * c:2 * c + 2, 0] = r["nk"]
        ncv[2 * c:2 * c + 2, 0] = r["nv"]
        nf[2 * c:2 * c + 2, 0] = r["nsf"].transpose(0, 2, 1, 3)
        nb[2 * c:2 * c + 2, 0] = r["nsb"].transpose(0, 2, 1, 3)
    return (y_prompt, y_sample, nck, ncv, nf, nb)


def kernel(**inputs):
    return assemble(run_device(inputs))
```

```python
import contextlib
import numpy as np
import concourse.bass as bass
import concourse.mybir as mybir
from concourse.bass_utils import run_bass_kernel_spmd

F32 = mybir.dt.float32
BF16 = mybir.dt.bfloat16
AF = mybir.ActivationFunctionType
ALU = mybir.AluOpType
AX = mybir.AxisListType

D = 1024
NT_S = 32
FULL_S = 17
EPS = 1e-6
SCALE = 128.0 ** -0.5
ENG_ATTR = {"pe": "tensor", "act": "scalar", "dve": "vector", "pool": "gpsimd", "sp": "sync"}
NDSEM = 96


class Sched:
    def __init__(self, nc, esems, dsems):
        self.nc = nc
        self.esems = esems
        self.dsems = dsems
        self.ops = []
        self.slot = 0.0
        self.ecnt = {k: 0 for k in esems}
        self.dcnt = [0] * len(dsems)
        self.pool_ids = list(range(0, 24))
        self.sp_ids = list(range(24, len(dsems)))
        self.seen = {e: {} for e in ENG_ATTR}

    def op(self, eng, fn, reads=(), writes=(), dma=False, sk=None):
        if dma and sk is None:
            sk = reads[0] if len(reads) else writes[0]
        self.ops.append((eng, fn, tuple(reads), tuple(writes), dma, sk, self.slot))

    def emit_phase(self):
        ops = [o[:6] for o in sorted(self.ops, key=lambda o: o[6])]
        self.ops = []
        self.slot = 0.0
        n = len(ops)
        last_w, readers = {}, {}
        deps = [None] * n
        has_dep = [False] * n
        for i, (eng, fn, rd, wr, dma, sk) in enumerate(ops):
            d = set()
            for r in rd:
                if r in last_w:
                    d.add(last_w[r])
                if isinstance(r, str) and len(r) > 1 and r[0] == "p" and r[1].isupper():
                    d.update(x for x in readers.get(r, ()) if ops[x][0] != eng)
            for w in wr:
                if w in last_w:
                    d.add(last_w[w])
                d.update(readers.get(w, ()))
            d.discard(i)
            d = {x for x in d if not (eng == "pe" and ops[x][0] == "pe" and not ops[x][4] and not dma)}
            deps[i] = d
            for x in d:
                has_dep[x] = True
            for w in wr:
                last_w[w] = i
                readers[w] = []
            for r in rd:
                if r not in wr:
                    readers.setdefault(r, []).append(i)
        last_of = {}
        for i, (eng, fn, rd, wr, dma, sk) in enumerate(ops):
            if dma:
                has_dep[i] = True
            else:
                last_of[eng] = i
        for i in last_of.values():
            has_dep[i] = True
        semof = [None] * n
        val = [0] * n
        dmap = {}
        for i, (eng, fn, rd, wr, dma, sk) in enumerate(ops):
            if not has_dep[i]:
                continue
            if dma:
                if (eng, sk) not in dmap:
                    pool_ids = self.pool_ids if eng == "pool" else self.sp_ids
                    used = sum(1 for (e2, _k) in dmap if (e2 == "pool") == (eng == "pool"))
                    assert used < len(pool_ids), "out of dma semaphores"
                    dmap[(eng, sk)] = pool_ids[used]
                j = dmap[(eng, sk)]
                self.dcnt[j] += 16
                semof[i] = ("d", j)
                val[i] = self.dcnt[j]
            else:
                self.ecnt[eng] += 1
                semof[i] = ("e", eng)
                val[i] = self.ecnt[eng]
        final = {("e", k): v for k, v in self.ecnt.items()}
        for j in dmap.values():
            final[("d", j)] = self.dcnt[j]

        def semh(k):
            return self.esems[k[1]] if k[0] == "e" else self.dsems[k[1]]

        def mk(engname):
            def body(e):
                seen = self.seen[engname]
                for i, (eng, fn, rd, wr, dma, sk) in enumerate(ops):
                    if eng != engname:
                        continue
                    need = {}
                    for x in deps[i]:
                        k = semof[x]
                        need[k] = max(need.get(k, 0), val[x])
                    for k, v in need.items():
                        if seen.get(k, 0) < v:
                            e.wait_ge(semh(k), v)
                            seen[k] = v
                    ins = fn(e)
                    if semof[i] is not None:
                        ins.then_inc(semh(semof[i]), 16 if dma else 1)
                for k, v in final.items():
                    if v > 0 and seen.get(k, 0) < v:
                        e.wait_ge(semh(k), v)
                        seen[k] = v
            return body

        with self.nc.Block() as block:
            for engname, attr in ENG_ATTR.items():
                getattr(block, attr)(mk(engname))


class Rot:
    def __init__(self, bufs, name):
        self.bufs = bufs
        self.name = name
        self.i = 0

    def next(self):
        j = self.i % len(self.bufs)
        self.i += 1
        return self.bufs[j], "%s#%d" % (self.name, j)


def build_nc(flags):
    nc = bass.Bass("TRN2", target_bir_lowering=False)

    def din(name, shape, dt=F32):
        return nc.dram_tensor(name, list(shape), dt, kind="ExternalInput").ap()

    def dout(name, shape, dt=F32):
        return nc.dram_tensor(name, list(shape), dt, kind="ExternalOutput").ap()

    def dscr(name, shape, dt):
        return nc.dram_tensor(name, list(shape), dt, kind="Internal").ap()

    xs = din("xs", [4096, D])
    xp = din("xp", [512, D])
    cT = din("cT", [128, 8, 2])
    ck = din("ck", [2, 512, 128])
    cv = din("cv", [2, 512, 128])
    s0f = din("s0f", [64, 8, 128])
    s0b = din("s0b", [64, 8, 128])
    dec = din("dec", [1, 32])
    w_in = din("w_in", [128, 8, 6656])
    w_mod = din("w_mod", [128, 8, 6144])
    b_mod = din("b_mod", [1, 6144])
    w_ao = din("w_ao", [128, 8, 1024])
    w_ro = din("w_ro", [128, 8, 1024])
    w_o = din("w_o", [128, 8, 1024])
    w_up = din("w_up", [128, 8, 5632])
    w_dn = din("w_dn", [128, 22, 1024])
    n1g = din("n1g", [1, D])
    n2g = din("n2g", [1, D])
    fg = din("fg", [1, D])
    qg = din("qg", [1, 128])
    kg = din("kg", [1, 128])
    convs = din("convs", [128, 4, 44])
    convp = din("convp", [128, 4, 44])
    rope_att = din("rope_att", [4096, 2, 128])
    rope_rs = din("rope_rs", [4096, 2, 64])
    rope_rp = din("rope_rp", [256, 2, 64])
    tabs = din("tabs", [128, 4, 128])
    tabe = din("tabe", [128, 4])

    ys = dout("ys", [2048, D])
    yp = dout("yp", [512, D])
    nk = dout("nk", [2, 2, 256, 128])
    nv = dout("nv", [2, 2, 256, 128])
    nsf = dout("nsf", [2, 64, 8, 128])
    nsb = dout("nsb", [2, 64, 8, 128])

    NFT = FULL_S + 4
    hT_scr = dscr("hT_scr", [NFT, 128, 1024], BF16)
    ya_scr = dscr("ya_scr", [NFT, 128, 1024], BF16)
    yr_scr = dscr("yr_scr", [NFT, 128, 1024], BF16)
    h2_scr = dscr("h2_scr", [NFT, 128, 1024], BF16)
    xs_scr = dscr("xs_scr", [NFT, 128, 1024], F32)
    mod_scr = dscr("mod_scr", [2, 6, 128, 1024], F32)

    groups = []
    groups.append(dict(name="S", samp=True, cond=0, tb=0, x=xs, nt=NT_S, full=list(range(FULL_S)), own=16,
                       sbase=0, nkt=36, kofs=0, rr=rope_rs, ra=rope_att))
    for p in range(2):
        groups.append(dict(name="P%d" % p, samp=False, cond=1, tb=1, x=xp[p * 256:(p + 1) * 256, :], nt=2,
                           full=[0, 1], own=2, sbase=FULL_S + 2 * p, nkt=2, kofs=0, rr=rope_rp, ra=None, p=p))

    with contextlib.ExitStack() as top:
        esems = {k: top.enter_context(nc.semaphore("sem_" + k)) for k in ENG_ATTR}
        dsems = [top.enter_context(nc.semaphore("dsem%d" % i)) for i in range(NDSEM)]
        S = Sched(nc, esems, dsems)
        slotbase = [0.0]

        def V(fn, r=(), w=()):
            S.op("dve", fn, r, w)

        def A(fn, r=(), w=()):
            S.op("act", fn, r, w)

        def G(fn, r=(), w=()):
            S.op("pool", fn, r, w)

        def T(fn, r=(), w=()):
            S.op("pe", fn, r, w)

        def DS(fn, r=(), w=(), sk=None):
            S.op("sp", fn, r, w, dma=True, sk=sk)

        def DG(fn, r=(), w=()):
            S.op("pool", fn, r, w, dma=True)

        def psb(name, shape, dt):
            return top.enter_context(nc.sbuf_tensor(name, list(shape), dt))

        identf = psb("identf", [128, 128], F32)
        ident = psb("ident", [128, 128], BF16)
        ones_bf = psb("ones_bf", [128, 128], BF16)
        negC = psb("negC", [128, 1], F32)
        QDF = [psb("QDF%d" % i, [128, 8], F32) for i in range(2)]
        QDB = [psb("QDB%d" % i, [128, 8], F32) for i in range(2)]
        KDF = [psb("KDF%d" % i, [128, 8], F32) for i in range(2)]
        KDB = [psb("KDB%d" % i, [128, 8], F32) for i in range(2)]
        CDF = [psb("CDF%d" % i, [128, 8], F32) for i in range(2)]
        CDB = [psb("CDB%d" % i, [128, 8], F32) for i in range(2)]
        qgb = psb("qgb", [128, 128], F32)
        kgb = psb("kgb", [128, 128], F32)
        sbst = contextlib.ExitStack()
        MT = [sbst.enter_context(nc.sbuf_tensor("MT%d" % i, [128, 1024], F32)) for i in range(2)]
        SBs = sbst.enter_context(nc.sbuf_tensor("SBs", [128, FULL_S + 4, 512], BF16))
        kvst = contextlib.ExitStack()
        KT = kvst.enter_context(nc.sbuf_tensor("KT", [128, 2, 36 * 128], BF16))
        Vb = kvst.enter_context(nc.sbuf_tensor("Vb", [128, 36, 256], BF16))
        KTp = [kvst.enter_context(nc.sbuf_tensor("KTp%d" % p, [128, 2, 256], BF16)) for p in range(2)]
        Vp = [kvst.enter_context(nc.sbuf_tensor("Vp%d" % p, [128, 2, 256], BF16)) for p in range(2)]

        def scr_tile(scr, idx):
            return scr[idx]

        with contextlib.ExitStack() as st:
            def sb(name, shape, dt):
                return st.enter_context(nc.sbuf_tensor(name, list(shape), dt))

            def ps(name, shape, dt):
                return st.enter_context(nc.psum_tensor(name, list(shape), dt))

            tabs_t = sb("tabs_t", [128, 4, 128], F32)
            tabe_t = sb("tabe_t", [128, 4], F32)
            decb = sb("decb", [128, 32], F32)
            lg = sb("lg", [128, 32], F32)
            tmp1 = sb("tmp1", [128, 128], F32)
            tmp2 = sb("tmp2", [128, 128], F32)
            tsm = sb("tsm", [128, 8], F32)
            cTt = sb("cTt", [128, 8, 2], F32)
            scb = [sb("scb%d" % j, [128, 8, 128], BF16) for j in range(2)]
            bmb = sb("bmb", [128, 6144], F32)
            n1gb = sb("n1gb", [128, D], F32)
            n2gb = sb("n2gb", [128, D], F32)
            wm = [sb("wm%d" % i, [128, 8, 512], BF16) for i in range(2)]
            mtmp = [sb("mtmp%d" % i, [128, 512], F32) for i in range(4)]
            ckt = sb("ckt", [128, 8, 128], F32)
            cvt = sb("cvt", [128, 8, 128], F32)
            ckb = sb("ckb", [128, 8, 128], BF16)
            sq = sb("sq", [128, 8, 128], F32)
            cs = sb("cs", [128, 8], F32)
            cm = sb("cm", [128, 4], F32)
            row = sb("row", [1, 128], F32)
            onesf = sb("onesf", [1, 128], F32)
            pM = [ps("pM%d" % i, [128, 512], F32) for i in range(2)]
            pTk = ps("pTk", [128, 8, 128], BF16)
            pR = ps("pR", [1, 128], F32)
            pC = ps("pC", [128, 1], F32)

            DS(lambda e: e.dma_start(out=tabs_t[:], in_=tabs), w=["tabs"])
            DS(lambda e: e.dma_start(out=tabe_t[:], in_=tabe), w=["tabe"])
            DS(lambda e: e.dma_start(out=decb[:], in_=dec.partition_broadcast(128)), w=["decb"])
            DS(lambda e: e.dma_start(out=cTt[:], in_=cT), w=["cTt"])
            DS(lambda e: e.dma_start(out=qgb[:], in_=qg.partition_broadcast(128)), w=["qgb"])
            DS(lambda e: e.dma_start(out=kgb[:], in_=kg.partition_broadcast(128)), w=["kgb"])
            DS(lambda e: e.dma_start(out=bmb[:], in_=b_mod.partition_broadcast(128)), w=["bmb"])
            DS(lambda e: e.dma_start(out=n1gb[:], in_=n1g.partition_broadcast(128)), w=["n1gb"])
            DS(lambda e: e.dma_start(out=n2gb[:], in_=n2g.partition_broadcast(128)), w=["n2gb"])
            DS(lambda e: e.dma_start(out=ckt[:].rearrange("p (k t) d -> p k t d", k=2),
                                     in_=ck.rearrange("k (t p) d -> p k t d", p=128)), w=["ckt"])
            DS(lambda e: e.dma_start(out=cvt[:].rearrange("p (k t) d -> p k t d", k=2),
                                     in_=cv.rearrange("k (t p) d -> p k t d", p=128)), w=["cvt"])
            G(lambda e: e.memset(identf[:], 0.0), w=["identf"])
            G(lambda e: e.affine_select(out=identf[:], in_=identf[:], pattern=[[-1, 128]], compare_op=ALU.not_equal,
                                        fill=1.0, base=0, channel_multiplier=1), r=["identf"], w=["identf"])
            V(lambda e: e.tensor_copy(out=ident[:], in_=identf[:]), r=["identf"], w=["ident"])
            G(lambda e: e.memset(ones_bf[:], 1.0), w=["ones_bf"])
            G(lambda e: e.memset(onesf[:], 1.0), w=["onesf"])
            A(lambda e: e.activation(out=lg[:], in_=decb[:], func=AF.Exp, scale=-1.0), r=["decb"], w=["lg"])
            A(lambda e: e.activation(out=lg[:], in_=lg[:], func=AF.Ln, bias=1.0), r=["lg"], w=["lg"])
            V(lambda e: e.tensor_scalar(out=lg[:], in0=lg[:], scalar1=-1.0, scalar2=None, op0=ALU.mult), r=["lg"], w=["lg"])
            for tb in range(2):
                lf = lambda h, tb=tb: lg[:, 16 * tb + h:16 * tb + h + 1]
                lb = lambda h, tb=tb: lg[:, 16 * tb + 8 + h:16 * tb + 8 + h + 1]
                for h in range(8):
                    A(lambda e, h=h, lf=lf: e.activation(out=tmp1[:], in_=tabs_t[:, 0, :], func=AF.Exp, scale=lf(h)),
                      r=["tabs", "lg", "tmp1"], w=["tmp1"])
                    V(lambda e: e.tensor_tensor(out=tmp1[:], in0=tmp1[:], in1=tabs_t[:, 1, :], op=ALU.mult), r=["tmp1", "tabs"], w=["tmp1"])
                    A(lambda e, h=h, lb=lb: e.activation(out=tmp2[:], in_=tabs_t[:, 2, :], func=AF.Exp, scale=lb(h)),
                      r=["tabs", "lg", "tmp2"], w=["tmp2"])
                    V(lambda e: e.tensor_tensor(out=tmp2[:], in0=tmp2[:], in1=tabs_t[:, 3, :], op=ALU.mult), r=["tmp2", "tabs"], w=["tmp2"])
                    V(lambda e: e.tensor_tensor(out=tmp1[:], in0=tmp1[:], in1=tmp2[:], op=ALU.add), r=["tmp1", "tmp2"], w=["tmp1"])
                    V(lambda e, h=h, tb=tb: e.tensor_scalar(out=MT[tb][:, h * 128:(h + 1) * 128], in0=tmp1[:], scalar1=0.125, scalar2=None,
                                                            op0=ALU.mult), r=["tmp1"], w=["MT"])
                lgf = lg[:, 16 * tb:16 * tb + 8]
                lgb = lg[:, 16 * tb + 8:16 * tb + 16]
                for (dst, src, col, mul) in ((QDF, lgf, 0, 1.0), (QDB, lgb, 1, 1.0), (KDF, lgf, 2, 0.125), (KDB, lgb, 3, 0.125)):
                    V(lambda e, src=src, col=col: e.tensor_scalar(out=tsm[:], in0=src, scalar1=tabe_t[:, col:col + 1], scalar2=None, op0=ALU.mult),
                      r=["lg", "tabe", "tsm"], w=["tsm"])
                    A(lambda e, dst=dst, tb=tb: e.activation(out=dst[tb][:], in_=tsm[:], func=AF.Exp), r=["tsm"], w=["dtab"])
                    if mul != 1.0:
                        V(lambda e, dst=dst, tb=tb, mul=mul: e.tensor_scalar(out=dst[tb][:], in0=dst[tb][:], scalar1=mul, scalar2=None, op0=ALU.mult),
                          r=["dtab"], w=["dtab"])
                A(lambda e, tb=tb, lgf=lgf: e.activation(out=CDF[tb][:], in_=lgf, func=AF.Exp, scale=128.0), r=["lg"], w=["cd"])
                A(lambda e, tb=tb, lgb=lgb: e.activation(out=CDB[tb][:], in_=lgb, func=AF.Exp, scale=128.0), r=["lg"], w=["cd"])
            V(lambda e: e.tensor_tensor(out=sq[:], in0=ckt[:], in1=ckt[:], op=ALU.mult), r=["ckt"], w=["sq"])
            V(lambda e: e.tensor_reduce(out=cs[:], in_=sq[:], axis=AX.X, op=ALU.add), r=["sq"], w=["cs"])
            V(lambda e: e.tensor_reduce(out=cm[:, 0:1], in_=cs[:], axis=AX.X, op=ALU.max), r=["cs"], w=["cm0"])
            T(lambda e: e.transpose(out=pR[:], in_=cm[:, 0:1], identity=identf[:]), r=["cm0", "identf"], w=["pR"])
            V(lambda e: e.tensor_reduce(out=row[:, 0:1], in_=pR[:], axis=AX.X, op=ALU.max), r=["pR"], w=["row"])
            T(lambda e: e.matmul(pC[:], lhsT=onesf[:], rhs=row[:, 0:1], start=True, stop=True), r=["onesf", "row"], w=["pC"])
            A(lambda e: e.activation(out=cm[:, 1:2], in_=pC[:], func=AF.Sqrt), r=["pC"], w=["cm1"])
            V(lambda e: e.tensor_reduce(out=cm[:, 2:3], in_=kgb[:], axis=AX.X, op=ALU.max, apply_absolute_value=True), r=["kgb"], w=["cm2"])
            V(lambda e: e.tensor_scalar(out=cm[:, 2:3], in0=cm[:, 2:3], scalar1=float(np.sqrt(128.0)), scalar2=None, op0=ALU.mult), r=["cm2"], w=["cm2"])
            V(lambda e: e.tensor_tensor(out=cm[:, 1:2], in0=cm[:, 1:2], in1=cm[:, 2:3], op=ALU.max), r=["cm1", "cm2"], w=["cm1"])
            V(lambda e: e.tensor_reduce(out=cm[:, 3:4], in_=qgb[:], axis=AX.X, op=ALU.max, apply_absolute_value=True), r=["qgb"], w=["cm3"])
            V(lambda e: e.tensor_tensor(out=cm[:, 1:2], in0=cm[:, 1:2], in1=cm[:, 3:4], op=ALU.mult), r=["cm1", "cm3"], w=["cm1"])
            V(lambda e: e.tensor_scalar(out=negC[:], in0=cm[:, 1:2], scalar1=-1.0, scalar2=None, op0=ALU.mult), r=["cm1"], w=["negC"])
            V(lambda e: e.tensor_copy(out=ckb[:], in_=ckt[:]), r=["ckt"], w=["ckb"])
            for j in range(8):
                T(lambda e, j=j: e.transpose(out=pTk[:, j, :], in_=ckb[:, j, :], identity=ident[:]), r=["ckb", "ident"], w=["pTk"])
            A(lambda e: e.copy(out=KT[:, :, 32 * 128:36 * 128], in_=pTk[:].rearrange("p (k t) d -> p k (t d)", k=2)), r=["pTk"], w=["KTc"])
            for kvh in range(2):
                V(lambda e, kvh=kvh: e.tensor_copy(out=Vb[:, 32:36, kvh * 128:(kvh + 1) * 128], in_=cvt[:, kvh * 4:(kvh + 1) * 4, :]), r=["cvt"], w=["Vbc"])
            A(lambda e: e.activation(out=cTt[:], in_=cTt[:], func=AF.Silu), r=["cTt"], w=["cTt"])
            for j in range(2):
                V(lambda e, j=j: e.tensor_copy(out=scb[j][:], in_=cTt[:, :, j:j + 1].to_broadcast([128, 8, 128])), r=["cTt"], w=["scb%d" % j])
            for cg in range(12):
                wbuf, wkey = wm[cg % 2], "wm%d" % (cg % 2)
                DG(lambda e, cg=cg, wbuf=wbuf: e.dma_start(out=wbuf[:], in_=w_mod[:, :, cg * 512:(cg + 1) * 512]), w=[wkey])
                sec, half = cg // 2, cg % 2
                for j in range(2):
                    for k in range(8):
                        T(lambda e, j=j, k=k, wbuf=wbuf: e.matmul(pM[j][:], lhsT=scb[j][:, k, :], rhs=wbuf[:, k, :], start=(k == 0), stop=(k == 7)),
                          r=["scb%d" % j, wkey], w=["pM%d" % j])
                    mt_, mkey = mtmp[(2 * cg + j) % 4], "mtmp%d" % ((2 * cg + j) % 4)
                    V(lambda e, j=j, cg=cg, mt_=mt_: e.tensor_tensor(out=mt_[:], in0=pM[j][:], in1=bmb[:, cg * 512:(cg + 1) * 512], op=ALU.add),
                      r=["pM%d" % j, "bmb"], w=[mkey])
                    if sec in (1, 4):
                        gsrc, gk = (n1gb, "n1gb") if sec == 1 else (n2gb, "n2gb")
                        V(lambda e, mt_=mt_, gsrc=gsrc, half=half: e.scalar_tensor_tensor(out=mt_[:], in0=mt_[:], scalar=1.0, in1=gsrc[:, half * 512:(half + 1) * 512],
                                                                                         op0=ALU.add, op1=ALU.mult), r=[mkey, gk], w=[mkey])
                    DS(lambda e, j=j, sec=sec, half=half, mt_=mt_: e.dma_start(out=mod_scr[j, sec, :, half * 512:(half + 1) * 512], in_=mt_[:]),
                       r=[mkey], w=["mod_scr"])
            S.emit_phase()

        with contextlib.ExitStack() as st:
            def sb(name, shape, dt):
                return st.enter_context(nc.sbuf_tensor(name, list(shape), dt))

            def ps(name, shape, dt):
                return st.enter_context(nc.psum_tensor(name, list(shape), dt))

            wA = sb("wA", [128, 8, 2048], BF16)
            Gb = [sb("Gb%d" % j, [128, D], F32) for j in range(2)]
            shb = [sb("shb%d" % j, [128, D], F32) for j in range(2)]
            xt_r = Rot([sb("xtA%d" % i, [128, D], F32) for i in range(4)], "xtA")
            ra_r = Rot([sb("raA%d" % i, [128, 2, 128], F32) for i in range(6)], "raA")
            rr_r = Rot([sb("rrA%d" % i, [128, 2, 64], F32) for i in range(6)], "rrA")
            junk = sb("junkA", [128, D], F32)
            ss_r = Rot([sb("ssA%d" % i, [128, 4], F32) for i in range(4)], "ssA")
            tmpf_r = Rot([sb("tmpfA%d" % i, [128, D], F32) for i in range(2)], "tmpfA")
            hb_r = Rot([sb("hbA%d" % i, [128, D], BF16) for i in range(3)], "hbA")
            hT_r = Rot([sb("hTA%d" % i, [128, 8, 128], BF16) for i in range(2)], "hTA")
            knr_r = Rot([sb("knA%d" % i, [128, 2, 128], F32) for i in range(2)], "knA")
            kn_r = Rot([sb("knoA%d" % i, [128, 2, 128], F32) for i in range(2)], "knoA")
            vf_r = Rot([sb("vfA%d" % i, [128, 256], F32) for i in range(2)], "vfA")
            tA_r = Rot([sb("tAA%d" % i, [128, 8, 64], F32) for i in range(4)], "tAA")
            tB_r = Rot([sb("tBA%d" % i, [128, 8, 64], F32) for i in range(4)], "tBA")
            kb_r = Rot([sb("kbA%d" % i, [128, 2, 128], BF16) for i in range(3)], "kbA")
            rkr_r = Rot([sb("rkrA%d" % i, [128, 8, 64], F32) for i in range(2)], "rkrA")
            kdb_r = Rot([sb("kdbA%d" % i, [128, 8, 64], BF16) for i in range(3)], "kdbA")
            rvb_r = Rot([sb("rvbA%d" % i, [128, 1024], BF16) for i in range(3)], "rvbA")
            Sb = sb("SbA", [128, 4, 128], F32)
            pT_r = Rot([ps("pTA%d" % i, [128, 8, 128], BF16) for i in range(1)], "pTA")
            pP = Rot([ps("pPA%d" % i, [128, 512], F32) for i in range(4)], "pPA")
            pT2 = ps("pT2A", [128, 8, 128], BF16)
            pKV = [ps("pKVA%d" % i, [128, 4, 128], F32) for i in range(2)]

            DG(lambda e: e.dma_start(out=wA[:, :, 0:512], in_=w_in[:, :, 1024:1536]), w=["wA0"])
            for i in range(3):
                DG(lambda e, i=i: e.dma_start(out=wA[:, :, 512 * (i + 1):512 * (i + 2)], in_=w_in[:, :, 2048 + 512 * i:2048 + 512 * (i + 1)]), w=["wA%d" % (i + 1)])
            for j in range(2):
                DS(lambda e, j=j: e.dma_start(out=Gb[j][:], in_=mod_scr[j, 1]), w=["Gb"])
                DS(lambda e, j=j: e.dma_start(out=shb[j][:], in_=mod_scr[j, 0]), w=["shb"])

            def _grp(g):
                samp, tb, cond = g["samp"], g["tb"], g["cond"]
                KTg = KT if samp else KTp[g["p"]]
                order = list(reversed(range(g["nt"])))
                loaded = {}

                def load(t, g=g, samp=samp, loaded=loaded):
                    xt, xk = xt_r.next()
                    DS(lambda e, t=t, xt=xt: e.dma_start(out=xt[:], in_=g["x"][t * 128:(t + 1) * 128, :]), w=[xk])
                    rr, rrk = rr_r.next()
                    DS(lambda e, t=t, rr=rr: e.dma_start(out=rr[:], in_=g["rr"][t * 128:(t + 1) * 128]), w=[rrk])
                    ra, rak = None, None
                    if samp:
                        ra, rak = ra_r.next()
                        DS(lambda e, t=t, ra=ra: e.dma_start(out=ra[:], in_=g["ra"][t * 128:(t + 1) * 128]), w=[rak])
                    loaded[t] = (xt, xk, rr, rrk, ra, rak)

                if samp:
                    s0v = s0b.rearrange("p (a b) d -> p a b d", b=2)
                    DS(lambda e: e.dma_start(out=Sb[0:64], in_=s0v[:, :, 0, :]), w=["Sb"])
                    DS(lambda e: e.dma_start(out=Sb[64:128], in_=s0v[:, :, 1, :]), w=["Sb"])
                else:
                    G(lambda e: e.memset(Sb[:], 0.0), w=["Sb"])
                base = slotbase[0]
                slotbase[0] += len(order) + 8
                for oi, t in enumerate(order):
                    S.slot = base + oi - 2
                    load(t)
                    S.slot = base + oi
                    xt, xk, rr, rrk, ra, rak = loaded.pop(t)
                    full = t in g["full"]
                    ss, ssk = ss_r.next()
                    tmpf, tmpfk = tmpf_r.next()
                    A(lambda e, xt=xt, ss=ss: e.activation(out=junk[:], in_=xt[:], func=AF.Square, scale=float(D ** -0.5), accum_out=ss[:, 0:1]), r=[xk], w=[ssk + "a"])
                    A(lambda e, ss=ss: e.activation(out=ss[:, 0:1], in_=ss[:, 0:1], func=AF.Sqrt, bias=EPS), r=[ssk + "a"], w=[ssk + "a"])
                    V(lambda e, ss=ss: e.reciprocal(out=ss[:, 0:1], in_=ss[:, 0:1]), r=[ssk + "a"], w=[ssk + "a"])
                    V(lambda e, xt=xt, cond=cond, ss=ss, tmpf=tmpf: e.scalar_tensor_tensor(out=tmpf[:], in0=xt[:], scalar=ss[:, 0:1], in1=Gb[cond][:], op0=ALU.mult, op1=ALU.mult),
                      r=[xk, ssk + "a", "Gb"], w=[tmpfk])
                    hb, hbk = hb_r.next()
                    G(lambda e, hb=hb, cond=cond, tmpf=tmpf: e.tensor_tensor(out=hb[:], in0=tmpf[:], in1=shb[cond][:], op=ALU.add), r=[tmpfk, "shb"], w=[hbk])
                    S.slot = base + oi + 1
                    pT, pTk = pT_r.next()
                    for k in range(8):
                        T(lambda e, k=k, hb=hb, pT=pT: e.transpose(out=pT[:, k, :], in_=hb[:, k * 128:(k + 1) * 128], identity=ident[:]), r=[hbk], w=[pTk])
                    hT, hTk = hT_r.next()
                    A(lambda e, hT=hT, pT=pT: e.copy(out=hT[:], in_=pT[:]), r=[pTk], w=[hTk])
                    if full:
                        si = g["sbase"] + g["full"].index(t)
                        DS(lambda e, hT=hT, si=si: e.dma_start(out=hT_scr[si], in_=hT[:].rearrange("p k d -> p (k d)")), r=[hTk], w=["hT_scr"])
                    pps = []
                    for cgi in range(4):
                        S.slot = base + oi + 2
                        pp, ppk = pP.next()
                        for k in range(8):
                            T(lambda e, k=k, cgi=cgi, pp=pp, hT=hT: e.matmul(pp[:], lhsT=hT[:, k, :], rhs=wA[:, k, cgi * 512:(cgi + 1) * 512], start=(k == 0), stop=(k == 7)),
                              r=[hTk, "wA%d" % cgi], w=[ppk])
                        pps.append((pp, ppk))
                        S.slot = base + oi + 2
                        if cgi == 0:
                            for hh in range(2):
                                A(lambda e, hh=hh, pp=pp, ss=ss: e.activation(out=junk[:, 0:128], in_=pp[:, hh * 128:(hh + 1) * 128], func=AF.Square, scale=float(128 ** -0.5),
                                                                           accum_out=ss[:, 1 + hh:2 + hh]), r=[ppk], w=[ssk + "b"])
                            A(lambda e, ss=ss: e.activation(out=ss[:, 1:3], in_=ss[:, 1:3], func=AF.Sqrt, bias=EPS), r=[ssk + "b"], w=[ssk + "b"])
                            V(lambda e, ss=ss: e.reciprocal(out=ss[:, 1:3], in_=ss[:, 1:3]), r=[ssk + "b"], w=[ssk + "b"])
                            kb, kbk = kb_r.next()
                            if samp:
                                kn, knk = knr_r.next()
                                for hh in range(2):
                                    V(lambda e, hh=hh, pp=pp, ss=ss, kn=kn: e.scalar_tensor_tensor(out=kn[:, hh, :], in0=pp[:, hh * 128:(hh + 1) * 128], scalar=ss[:, 1 + hh:2 + hh],
                                                                                                   in1=kgb[:], op0=ALU.mult, op1=ALU.mult), r=[ppk, ssk + "b", "kgb"], w=[knk])
                                tA, tAk = tA_r.next()
                                tB, tBk = tB_r.next()
                                S.slot = base + oi + 3
                                rope_att_ops(V, G, kn, knk, ra, rak, kb, kbk, 2, tA, tB, tAk, tBk)
                                S.slot = base + oi + 2
                            else:
                                kno, knok = kn_r.next()
                                for hh in range(2):
                                    V(lambda e, hh=hh, pp=pp, kno=kno, ss=ss: e.scalar_tensor_tensor(out=kno[:, hh, :], in0=pp[:, hh * 128:(hh + 1) * 128], scalar=ss[:, 1 + hh:2 + hh],
                                                                                                     in1=kgb[:], op0=ALU.mult, op1=ALU.mult), r=[ppk, ssk + "b", "kgb"], w=[knok])
                                DS(lambda e, kno=kno, t=t, g=g: e.dma_start(out=nk[g["p"], :, t * 128:(t + 1) * 128, :].rearrange("k p d -> p k d"), in_=kno[:]), r=[knok], w=["nk"])
                                S.slot = base + oi + 3
                                G(lambda e, kb=kb, kno=kno: e.tensor_copy(out=kb[:], in_=kno[:]), r=[knok], w=[kbk])
                            S.slot = base + oi + 4
                            for hh in range(2):
                                T(lambda e, hh=hh, kb=kb: e.transpose(out=pT2[:, hh, :], in_=kb[:, hh, :], identity=ident[:]), r=[kbk], w=["pT2A"])
                            A(lambda e, t=t, KTg=KTg: e.copy(out=KTg[:, :, t * 128:(t + 1) * 128], in_=pT2[:, 0:2, :]), r=["pT2A"], w=["KT"])
                            S.slot = base + oi + 2
                            if samp:
                                A(lambda e, t=t, pp=pp: e.copy(out=Vb[:, t, :], in_=pp[:, 256:512]), r=[ppk], w=["Vb"])
                            else:
                                vf, vfk = vf_r.next()
                                A(lambda e, vf=vf, pp=pp: e.copy(out=vf[:], in_=pp[:, 256:512]), r=[ppk], w=[vfk])
                                DS(lambda e, vf=vf, t=t, g=g: e.dma_start(out=nv[g["p"], :, t * 128:(t + 1) * 128, :].rearrange("k p d -> p k d"),
                                                                          in_=vf[:].rearrange("p (k d) -> p k d", k=2)), r=[vfk], w=["nv"])
                                G(lambda e, vf=vf, t=t, g=g: e.tensor_copy(out=Vp[g["p"]][:, t, :], in_=vf[:]), r=[vfk], w=["Vb"])
                    (pk, pkk), (pv0, pv0k), (pv1, pv1k) = pps[1], pps[2], pps[3]
                    rkr, rkrk = rkr_r.next()
                    tA, tAk = tA_r.next()
                    tB, tBk = tB_r.next()
                    rope_ret_ops(V, G, pk[:].rearrange("p (h d) -> p h d", h=8), pkk, rr, rrk, rkr, rkrk, tA, tB, tAk, tBk)
                    kdb, kdbk = kdb_r.next()
                    S.slot = base + oi + 3
                    V(lambda e, kdb=kdb, tb=tb, rkr=rkr: e.tensor_tensor(out=kdb[:], in0=rkr[:], in1=KDB[tb][:].unsqueeze(2).to_broadcast([128, 8, 64]), op=ALU.mult),
                      r=[rkrk], w=[kdbk])
                    S.slot = base + oi + 2
                    rvb, rvbk = rvb_r.next()
                    A(lambda e, rvb=rvb, pv0=pv0: e.copy(out=rvb[:, 0:512], in_=pv0[:]), r=[pv0k], w=[rvbk + "a"])
                    A(lambda e, rvb=rvb, pv1=pv1: e.copy(out=rvb[:, 512:1024], in_=pv1[:]), r=[pv1k], w=[rvbk + "b"])
                    S.slot = base + oi + 4
                    if full:
                        slot = g["sbase"] + g["full"].index(t)
                        G(lambda e, slot=slot: e.tensor_copy(out=SBs[:, slot, :], in_=Sb[:].rearrange("p a d -> p (a d)")), r=["Sb"], w=["SBs"])
                    last = (oi == len(order) - 1)
                    if samp and last:
                        continue
                    for h in range(8):
                        T(lambda e, h=h, kdb=kdb, rvb=rvb: e.matmul(pKV[h // 4][:, h % 4, :], lhsT=kdb[:, 2 * (h // 2):2 * (h // 2) + 2, :].rearrange("p a d -> p (a d)"),
                                                                    rhs=rvb[:, h * 128:(h + 1) * 128], start=True, stop=True),
                          r=[kdbk, rvbk + "a", rvbk + "b"], w=["pKVA%d" % (h // 4)])
                    cdv = CDB[tb][:].rearrange("p (a b) -> p a b", b=2)
                    for par in range(2):
                        lo = 64 * par
                        V(lambda e, par=par, lo=lo, cdv=cdv: e.tensor_tensor(out=Sb[lo:lo + 64], in0=Sb[lo:lo + 64],
                                                                              in1=cdv[lo:lo + 64, :, par].unsqueeze(2).to_broadcast([64, 4, 128]), op=ALU.mult), r=["Sb"], w=["Sb"])
                        for bk in range(2):
                            V(lambda e, par=par, lo=lo, bk=bk: e.tensor_tensor(out=Sb[lo:lo + 64, 2 * bk:2 * bk + 2, :], in0=Sb[lo:lo + 64, 2 * bk:2 * bk + 2, :],
                                                                                in1=pKV[bk][lo:lo + 64].rearrange("p (a b) d -> p a b d", b=2)[:, :, par, :], op=ALU.add),
                              r=["Sb", "pKVA%d" % bk], w=["Sb"])
                    if (not samp) and last:
                        nv_ = nsb[g["p"]].rearrange("p (a b) d -> p a b d", b=2)
                        for par in range(2):
                            DS(lambda e, par=par, nv_=nv_: e.dma_start(out=nv_[:, :, par, :], in_=Sb[64 * par:64 * par + 64]), r=["Sb"], w=["nsb"])
            for g_ in groups:
                _grp(g_)
            S.emit_phase()

        if flags.get("stop_after") == "A":
            kvst.close()
            sbst.close()
            return nc

        with contextlib.ExitStack() as st:
            def sb(name, shape, dt):
                return st.enter_context(nc.sbuf_tensor(name, list(shape), dt))

            def ps(name, shape, dt):
                return st.enter_context(nc.psum_tensor(name, list(shape), dt))

            wq = sb("wq", [128, 8, 1024], BF16)
            slotbase[0] = 0.0
            hTb_r = Rot([sb("hTbB%d" % i, [128, 4, 1024], BF16) for i in range(2)], "hTbB")
            raB_r = Rot([sb("raB%d" % i, [128, 4, 2, 128], F32) for i in range(2)], "raB")
            qT_r = Rot([sb("qTB%d" % i, [128, 8, 512], BF16) for i in range(2)], "qTB")
            ya_r = Rot([sb("yaB%d" % i, [128, 8, 512], BF16) for i in range(2)], "yaB")
            junk = sb("junkB", [128, 128], F32)
            ssq_r = Rot([sb("ssqB%d" % i, [128, 4], F32) for i in range(3)], "ssqB")
            qn_r = Rot([sb("qnB%d" % i, [128, 4, 128], F32) for i in range(2)], "qnB")
            qraw_r = Rot([sb("qrawB%d" % i, [128, 512], F32) for i in range(3)], "qrawB")
            tA_r = Rot([sb("tAB%d" % i, [128, 8, 64], F32) for i in range(2)], "tAB")
            tB_r = Rot([sb("tBB%d" % i, [128, 8, 64], F32) for i in range(2)], "tBB")
            qb_r = Rot([sb("qbB%d" % i, [128, 4, 128], BF16) for i in range(3)], "qbB")
            pe_r = Rot([sb("peB%d" % i, [128, 512], BF16) for i in range(6)], "peB")
            rs_r = Rot([sb("rsB%d" % i, [128, 512], F32) for i in range(2)], "rsB")
            pQ = ps("pQB", [128, 512], F32)
            pTq = ps("pTqB", [128, 8, 128], BF16)
            pS_r = Rot([ps("pSB%d" % i, [128, 512], F32) for i in range(3)], "pSB")
            pO_r = Rot([ps("pOB%d" % i, [128, 512], F32) for i in range(2)], "pOB")
            pZ_r = Rot([ps("pZB%d" % i, [128, 512], F32) for i in range(1)], "pZB")
            for i in range(2):
                DG(lambda e, i=i: e.dma_start(out=wq[:, :, i * 512:(i + 1) * 512], in_=w_in[:, :, i * 512:(i + 1) * 512]), w=["wq%d" % i])
            def _grp(g):
                samp = g["samp"]
                KTg = KT if samp else KTp[g["p"]]
                Vg = Vb if samp else Vp[g["p"]]
                nkt = g["nkt"]
                blocks = [g["full"][i:i + 4] for i in range(0, len(g["full"]), 4)]
                def _blk(blk):
                    bi = int(slotbase[0])
                    slotbase[0] += 1
                    nq = 128 * len(blk)
                    S.slot = 10 * bi - 8
                    hTb, hTbk = hTb_r.next()
                    rab, rabk = raB_r.next()
                    for j, t in enumerate(blk):
                        si = g["sbase"] + g["full"].index(t)
                        DS(lambda e, j=j, si=si, hTb=hTb: e.dma_start(out=hTb[:, j, :], in_=hT_scr[si]), w=[hTbk + "_%d" % j])
                        if samp:
                            DS(lambda e, j=j, t=t, rab=rab: e.dma_start(out=rab[:, j], in_=g["ra"][t * 128:(t + 1) * 128]), w=[rabk + "_%d" % j])
                    qT, qTk = qT_r.next()
                    for j, t in enumerate(blk):
                        for hf in range(2):
                            u = 2 * j + hf
                            S.slot = 10 * bi + u + 0.5
                            ssq, ssqk = ssq_r.next()
                            qn, qnk = qn_r.next()
                            for k in range(8):
                                T(lambda e, j=j, hf=hf, k=k, hTb=hTb: e.matmul(pQ[:], lhsT=hTb[:, j, k * 128:(k + 1) * 128], rhs=wq[:, k, hf * 512:(hf + 1) * 512],
                                                                               start=(k == 0), stop=(k == 7)), r=[hTbk + "_%d" % j, "wq%d" % hf], w=["pQB"])
                            qraw, qrawk = qraw_r.next()
                            V(lambda e, qraw=qraw: e.tensor_copy(out=qraw[:], in_=pQ[:]), r=["pQB"], w=[qrawk])
                            for hh in range(4):
                                V(lambda e, hh=hh, ssq=ssq, qraw=qraw: e.scalar_tensor_tensor(out=junk[:], in0=qraw[:, hh * 128:(hh + 1) * 128], scalar=1.0 / 128, in1=qraw[:, hh * 128:(hh + 1) * 128],
                                                                                              op0=ALU.mult, op1=ALU.mult, accum_out=ssq[:, hh:hh + 1]), r=[qrawk], w=[ssqk])
                            S.slot = 10 * bi + u + 1.5
                            G(lambda e, ssq=ssq: e.tensor_scalar(out=ssq[:], in0=ssq[:], scalar1=EPS, scalar2=None, op0=ALU.add), r=[ssqk], w=[ssqk])
                            A(lambda e, ssq=ssq: e.activation(out=ssq[:], in_=ssq[:], func=AF.Ln), r=[ssqk], w=[ssqk])
                            A(lambda e, ssq=ssq: e.activation(out=ssq[:], in_=ssq[:], func=AF.Exp, scale=-0.5), r=[ssqk], w=[ssqk])
                            for hh in range(4):
                                V(lambda e, hh=hh, ssq=ssq, qn=qn, qraw=qraw: e.scalar_tensor_tensor(out=qn[:, hh, :], in0=qraw[:, hh * 128:(hh + 1) * 128], scalar=ssq[:, hh:hh + 1], in1=qgb[:],
                                                                                                     op0=ALU.mult, op1=ALU.mult), r=[qrawk, ssqk], w=[qnk])
                            qb, qbk = qb_r.next()
                            if samp:
                                tA, tAk = tA_r.next()
                                tB, tBk = tB_r.next()
                                rope_att_ops(V, G, qn, qnk, rab[:, j], rabk + "_%d" % j, qb, qbk, 4, tA, tB, tAk, tBk)
                            else:
                                G(lambda e, qb=qb, qn=qn: e.tensor_copy(out=qb[:], in_=qn[:]), r=[qnk], w=[qbk])
                            S.slot = 10 * bi + u + 2.5
                            for hh in range(4):
                                T(lambda e, hh=hh, qb=qb: e.transpose(out=pTq[:, hh, :], in_=qb[:, hh, :], identity=ident[:]), r=[qbk], w=["pTqB"])
                            V(lambda e, j=j, hf=hf, qT=qT: e.tensor_copy(out=qT[:, hf * 4:(hf + 1) * 4, j * 128:(j + 1) * 128], in_=pTq[:, 0:4, :]), r=["pTqB"], w=[qTk + "_%d" % hf])
                    nqa = 32 if (samp and len(blk) == 1) else nq
                    ya, yak = ya_r.next()
                    for h in range(8):
                        S.slot = 10 * (bi + 1) + h
                        kvh = h // 4
                        pO, pOk = pO_r.next()
                        pZ, pZk = pZ_r.next()
                        LA = 2
                        pending = []
                        for kt in range(nkt + LA):
                            if kt < nkt:
                                pS, pSk = pS_r.next()
                                T(lambda e, kt=kt, pS=pS, qT=qT, h=h, kvh=kvh: e.matmul(pS[:, 0:nqa], lhsT=KTg[:, kvh, kt * 128:(kt + 1) * 128], rhs=qT[:, h, 0:nqa],
                                                                                        start=True, stop=True), r=[qTk + "_%d" % (h // 4), "KT"], w=[pSk])
                                pe, pek = pe_r.next()
                                A(lambda e, pS=pS, pe=pe: e.activation(out=pe[:, 0:nqa], in_=pS[:, 0:nqa], func=AF.Exp, scale=SCALE, bias=negC[:, 0:1]), r=[pSk], w=[pek])
                                pending.append((kt, pe, pek))
                            if kt >= LA:
                                k0, pe0, pek0 = pending.pop(0)
                                T(lambda e, k0=k0, pe0=pe0, pO=pO, kvh=kvh: e.matmul(pO[:, 0:nqa], lhsT=Vg[:, k0, kvh * 128:(kvh + 1) * 128], rhs=pe0[:, 0:nqa],
                                                                                     start=(k0 == 0), stop=(k0 == nkt - 1)), r=[pek0, "Vb"], w=[pOk])
                                T(lambda e, k0=k0, pe0=pe0, pZ=pZ: e.matmul(pZ[:, 0:nqa], lhsT=ones_bf[:], rhs=pe0[:, 0:nqa], start=(k0 == 0), stop=(k0 == nkt - 1)),
                                  r=[pek0], w=[pZk])
                        rs, rsk = rs_r.next()
                        V(lambda e, rs=rs, pZ=pZ: e.reciprocal(out=rs[:, 0:nqa], in_=pZ[:, 0:nqa]), r=[pZk], w=[rsk])
                        V(lambda e, rs=rs, pO=pO, ya=ya, h=h: e.tensor_tensor(out=ya[:, h, 0:nqa], in0=pO[:, 0:nqa], in1=rs[:, 0:nqa], op=ALU.mult), r=[pOk, rsk], w=[yak])
                    S.slot = 10 * (bi + 1) + 9
                    for j, t in enumerate(blk):
                        si = g["sbase"] + g["full"].index(t)
                        DS(lambda e, j=j, si=si, ya=ya: e.dma_start(out=ya_scr[si].rearrange("p (h d) -> p h d", h=8), in_=ya[:, :, j * 128:(j + 1) * 128]), r=[yak], w=["ya_scr"])
                for blk_ in blocks:
                    _blk(blk_)
            for g_ in groups:
                _grp(g_)
            S.emit_phase()

        kvst.close()
        if flags.get("stop_after") == "B1":
            sbst.close()
            return nc
        with contextlib.ExitStack() as st:
            def sb(name, shape, dt):
                return st.enter_context(nc.sbuf_tensor(name, list(shape), dt))

            def ps(name, shape, dt):
                return st.enter_context(nc.psum_tensor(name, list(shape), dt))

            def rot(name, shape, dt, n):
                return Rot([sb("%s%d" % (name, i), shape, dt) for i in range(n)], name)

            slotbase[0] = 0.0
            wB = sb("wB", [128, 8, 3072], BF16)
            hT_r = rot("hTC", [128, 1024], BF16, 3)
            rr_r = rot("rrC", [128, 2, 64], F32, 3)
            tA_r = rot("tAC", [128, 8, 64], F32, 2)
            tB_r = rot("tBC", [128, 8, 64], F32, 2)
            raw_r = rot("rawC", [128, 512], F32, 2)
            rqb_r = rot("rqbC", [128, 8, 64], BF16, 2)
            rkb_r = rot("rkbC", [128, 8, 64], BF16, 2)
            qdf_r = rot("qdfC", [128, 8, 64], BF16, 2)
            qdb_r = rot("qdbC", [128, 8, 64], BF16, 2)
            kdf_r = rot("kdfC", [128, 8, 64], BF16, 4)
            rvb_r = rot("rvbC", [128, 1024], BF16, 4)
            sg_r = rot("sgC", [128, 1024], F32, 5)
            qkT_r = rot("qkTC", [128, 8, 128], BF16, 2)
            qdT_r = rot("qdTC", [128, 8, 128], BF16, 3)
            msk_r = rot("mskC", [128, 1024], BF16, 2)
            junk = sb("junkC", [128, 128], F32)
            ssr_r = rot("ssrC", [128, 8], F32, 2)
            yrb_r = rot("yrbC", [128, 1024], BF16, 2)
            yrT_r = rot("yrTC", [128, 1024], BF16, 2)
            SF = sb("SFC", [128, 4, 128], F32)
            SFb = sb("SFbC", [128, 4, 128], BF16)
            pP_r = Rot([ps("pPC%d" % i, [128, 512], F32) for i in range(3)], "pPC")
            pTa = ps("pTaC", [128, 8, 128], BF16)
            pSc = [ps("pScC%d" % i, [128, 4, 128], F32) for i in range(2)]
            pOr = [ps("pOrC%d" % i, [128, 4, 128], F32) for i in range(2)]
            for i in range(6):
                DG(lambda e, i=i: e.dma_start(out=wB[:, :, i * 512:(i + 1) * 512], in_=w_in[:, :, 1536 + i * 512:1536 + (i + 1) * 512]), w=["wB%d" % i])

            def _grp(g):
                samp, tb = g["samp"], g["tb"]
                fl = g["full"]
                base = slotbase[0]
                slotbase[0] += len(fl) + 10
                S.slot = base - 3
                if samp:
                    s0v = s0f.rearrange("p (a b) d -> p a b d", b=2)
                    DS(lambda e: e.dma_start(out=SF[0:64], in_=s0v[:, :, 0, :]), w=["SF"])
                    DS(lambda e: e.dma_start(out=SF[64:128], in_=s0v[:, :, 1, :]), w=["SF"])
                else:
                    G(lambda e: e.memset(SF[:], 0.0), w=["SF"])
                G(lambda e: e.tensor_copy(out=SFb[:], in_=SF[:]), r=["SF"], w=["SFb"])
                for oi, t in enumerate(fl):
                    slot = g["sbase"] + oi
                    S.slot = base + oi - 2
                    hT, hTk = hT_r.next()
                    DS(lambda e, slot=slot, hT=hT: e.dma_start(out=hT[:], in_=hT_scr[slot]), w=[hTk])
                    rr, rrk = rr_r.next()
                    DS(lambda e, t=t, rr=rr: e.dma_start(out=rr[:], in_=g["rr"][t * 128:(t + 1) * 128]), w=[rrk])
                    S.slot = base + oi
                    rqb, rqbk = rqb_r.next()
                    rkb, rkbk = rkb_r.next()
                    qdf, qdfk = qdf_r.next()
                    qdb, qdbk = qdb_r.next()
                    kdf, kdfk = kdf_r.next()
                    rvb, rvbk = rvb_r.next()
                    sg, sgk = sg_r.next()
                    for cgi in range(6):
                        pp, ppk = pP_r.next()
                        for k in range(8):
                            T(lambda e, k=k, cgi=cgi, pp=pp, hT=hT: e.matmul(pp[:], lhsT=hT[:, k * 128:(k + 1) * 128], rhs=wB[:, k, cgi * 512:(cgi + 1) * 512],
                                                                             start=(k == 0), stop=(k == 7)), r=[hTk, "wB%d" % cgi], w=[ppk])
                        if cgi == 0:
                            tA, tAk = tA_r.next()
                            tB, tBk = tB_r.next()
                            raw, rawk = raw_r.next()
                            A(lambda e, raw=raw, pp=pp: e.copy(out=raw[:], in_=pp[:]), r=[ppk], w=[rawk])
                            rope_ret_ops(V, G, raw[:].rearrange("p (h d) -> p h d", h=8), rawk, rr, rrk, rqb, rqbk, tA, tB, tAk, tBk)
                            S.slot = base + oi + 0.5
                            V(lambda e, tb=tb, qdf=qdf, rqb=rqb: e.tensor_tensor(out=qdf[:], in0=rqb[:], in1=QDF[tb][:].unsqueeze(2).to_broadcast([128, 8, 64]), op=ALU.mult), r=[rqbk], w=[qdfk])
                            V(lambda e, tb=tb, qdb=qdb, rqb=rqb: e.tensor_tensor(out=qdb[:], in0=rqb[:], in1=QDB[tb][:].unsqueeze(2).to_broadcast([128, 8, 64]), op=ALU.mult), r=[rqbk], w=[qdbk])
                            S.slot = base + oi
                        elif cgi == 1:
                            tA, tAk = tA_r.next()
                            tB, tBk = tB_r.next()
                            raw, rawk = raw_r.next()
                            A(lambda e, raw=raw, pp=pp: e.copy(out=raw[:], in_=pp[:]), r=[ppk], w=[rawk])
                            rope_ret_ops(V, G, raw[:].rearrange("p (h d) -> p h d", h=8), rawk, rr, rrk, rkb, rkbk, tA, tB, tAk, tBk)
                            S.slot = base + oi + 0.5
                            V(lambda e, tb=tb, kdf=kdf, rkb=rkb: e.tensor_tensor(out=kdf[:], in0=rkb[:], in1=KDF[tb][:].unsqueeze(2).to_broadcast([128, 8, 64]), op=ALU.mult), r=[rkbk], w=[kdfk])
                            S.slot = base + oi
                        elif cgi in (2, 3):
                            A(lambda e, cgi=cgi, pp=pp, rvb=rvb: e.copy(out=rvb[:, (cgi - 2) * 512:(cgi - 1) * 512], in_=pp[:]), r=[ppk], w=[rvbk + "_%d" % (cgi - 2)])
                        else:
                            A(lambda e, cgi=cgi, pp=pp, sg=sg: e.activation(out=sg[:, (cgi - 4) * 512:(cgi - 3) * 512], in_=pp[:], func=AF.Silu), r=[ppk], w=[sgk + "_%d" % (cgi - 4)])
                    rvbks = [rvbk + "_0", rvbk + "_1"]
                    S.slot = base + oi + 1
                    qkT, qkTk = qkT_r.next()
                    qdT, qdTk = qdT_r.next()
                    for hp in range(4):
                        T(lambda e, hp=hp, rqb=rqb: e.transpose(out=pTa[:, hp, :], in_=rqb[:, 2 * hp:2 * hp + 2, :].rearrange("p a d -> p (a d)"), identity=ident[:]), r=[rqbk], w=["pTaC"])
                        T(lambda e, hp=hp, rkb=rkb: e.transpose(out=pTa[:, 4 + hp, :], in_=rkb[:, 2 * hp:2 * hp + 2, :].rearrange("p a d -> p (a d)"), identity=ident[:]), r=[rkbk], w=["pTaC"])
                    A(lambda e, qkT=qkT: e.copy(out=qkT[:], in_=pTa[:]), r=["pTaC"], w=[qkTk])
                    for hp in range(4):
                        T(lambda e, hp=hp, qdf=qdf: e.transpose(out=pTa[:, hp, :], in_=qdf[:, 2 * hp:2 * hp + 2, :].rearrange("p a d -> p (a d)"), identity=ident[:]), r=[qdfk], w=["pTaC"])
                        T(lambda e, hp=hp, qdb=qdb: e.transpose(out=pTa[:, 4 + hp, :], in_=qdb[:, 2 * hp:2 * hp + 2, :].rearrange("p a d -> p (a d)"), identity=ident[:]), r=[qdbk], w=["pTaC"])
                    V(lambda e, qdT=qdT: e.tensor_copy(out=qdT[:], in_=pTa[:]), r=["pTaC"], w=[qdTk])
                    S.slot = base + oi + 2
                    msk, mskk = msk_r.next()
                    for h in range(8):
                        lo = 64 * (h % 2)
                        T(lambda e, h=h, lo=lo, qkT=qkT: e.matmul(pSc[h % 2][:, h // 2, :], lhsT=qkT[lo:lo + 64, 4 + h // 2, :], rhs=qkT[lo:lo + 64, h // 2, :], start=True, stop=True),
                          r=[qkTk], w=["pScC%d" % (h % 2)])
                    mskv = msk[:].rearrange("p (a b d) -> p a b d", a=4, b=2)
                    for par in range(2):
                        V(lambda e, par=par, tb=tb, mskv=mskv: e.tensor_tensor(out=mskv[:, :, par, :], in0=pSc[par][:],
                                                                                in1=MT[tb][:].rearrange("p (a b d) -> p a b d", a=4, b=2)[:, :, par, :], op=ALU.mult),
                          r=["pScC%d" % par], w=[mskk + "_%d" % par])
                    S.slot = base + oi + 3
                    ssr, ssrk = ssr_r.next()
                    last = (oi == len(fl) - 1)
                    do_upd = not (samp and last)
                    if do_upd:
                        for h in range(8):
                            T(lambda e, h=h, kdf=kdf, rvb=rvb: e.matmul(pSc[h // 4][:, h % 4, :], lhsT=kdf[:, 2 * (h // 2):2 * (h // 2) + 2, :].rearrange("p a d -> p (a d)"),
                                                                        rhs=rvb[:, h * 128:(h + 1) * 128], start=True, stop=True), r=[kdfk] + rvbks, w=["pScC%d" % (h // 4)])
                    for h in range(8):
                        lo = 64 * (h % 2)
                        T(lambda e, h=h, msk=msk, rvb=rvb: e.matmul(pOr[h % 2][:, h // 2, :], lhsT=msk[:, h * 128:(h + 1) * 128], rhs=rvb[:, h * 128:(h + 1) * 128], start=True, stop=False),
                          r=[mskk + "_%d" % (h % 2)] + rvbks, w=["pOrC%d" % (h % 2)])
                        T(lambda e, h=h, lo=lo, qdT=qdT: e.matmul(pOr[h % 2][:, h // 2, :], lhsT=qdT[lo:lo + 64, h // 2, :], rhs=SFb[lo:lo + 64, h // 2, :], start=False, stop=False),
                          r=[qdTk, "SFb"], w=["pOrC%d" % (h % 2)])
                        T(lambda e, h=h, lo=lo, slot=slot, qdT=qdT: e.matmul(pOr[h % 2][:, h // 2, :], lhsT=qdT[lo:lo + 64, 4 + h // 2, :],
                                                                             rhs=SBs[lo:lo + 64, slot, (h // 2) * 128:(h // 2 + 1) * 128], start=False, stop=True),
                          r=[qdTk], w=["pOrC%d" % (h % 2)])
                    if do_upd:
                        cdv = CDF[tb][:].rearrange("p (a b) -> p a b", b=2)
                        for par in range(2):
                            lo = 64 * par
                            V(lambda e, lo=lo, par=par, cdv=cdv: e.tensor_tensor(out=SF[lo:lo + 64], in0=SF[lo:lo + 64],
                                                                                  in1=cdv[lo:lo + 64, :, par].unsqueeze(2).to_broadcast([64, 4, 128]), op=ALU.mult), r=["SF"], w=["SF"])
                            for bk in range(2):
                                V(lambda e, lo=lo, par=par, bk=bk: e.tensor_tensor(out=SF[lo:lo + 64, 2 * bk:2 * bk + 2, :], in0=SF[lo:lo + 64, 2 * bk:2 * bk + 2, :],
                                                                                    in1=pSc[bk][lo:lo + 64].rearrange("p (a b) d -> p a b d", b=2)[:, :, par, :], op=ALU.add),
                                  r=["SF", "pScC%d" % bk], w=["SF"])
                        V(lambda e: e.tensor_copy(out=SFb[:], in_=SF[:]), r=["SF"], w=["SFb"])
                        if (not samp) and last:
                            nf_ = nsf[g["p"]].rearrange("p (a b) d -> p a b d", b=2)
                            for par in range(2):
                                DS(lambda e, par=par, nf_=nf_: e.dma_start(out=nf_[:, :, par, :], in_=SF[64 * par:64 * par + 64]), r=["SF"], w=["nsf"])
                    for h in range(8):
                        A(lambda e, h=h, ssr=ssr: e.activation(out=junk[:], in_=pOr[h % 2][:, h // 2, :], func=AF.Square, scale=float(128 ** -0.5), accum_out=ssr[:, h:h + 1]),
                          r=["pOrC%d" % (h % 2)], w=[ssrk])
                    A(lambda e, ssr=ssr: e.activation(out=ssr[:], in_=ssr[:], func=AF.Sqrt, bias=EPS), r=[ssrk], w=[ssrk])
                    S.slot = base + oi + 4
                    yrb, yrbk = yrb_r.next()
                    V(lambda e, ssr=ssr: e.reciprocal(out=ssr[:], in_=ssr[:]), r=[ssrk], w=[ssrk])
                    for h in range(8):
                        V(lambda e, h=h, ssr=ssr, yrb=yrb, sg=sg: e.scalar_tensor_tensor(out=yrb[:, h * 128:(h + 1) * 128], in0=pOr[h % 2][:, h // 2, :], scalar=ssr[:, h:h + 1],
                                                                                         in1=sg[:, h * 128:(h + 1) * 128], op0=ALU.mult, op1=ALU.mult),
                          r=["pOrC%d" % (h % 2), ssrk, sgk + "_0", sgk + "_1"], w=[yrbk])
                    S.slot = base + oi + 5
                    for k in range(8):
                        T(lambda e, k=k, yrb=yrb: e.transpose(out=pTa[:, k, :], in_=yrb[:, k * 128:(k + 1) * 128], identity=ident[:]), r=[yrbk], w=["pTaC"])
                    yrT, yrTk = yrT_r.next()
                    A(lambda e, yrT=yrT: e.copy(out=yrT[:], in_=pTa[:].rearrange("p k d -> p (k d)")), r=["pTaC"], w=[yrTk])
                    DS(lambda e, yrT=yrT, slot=slot: e.dma_start(out=yr_scr[slot], in_=yrT[:]), r=[yrTk], w=["yr_scr"])
            for g_ in groups:
                _grp(g_)
            S.emit_phase()

        sbst.close()
        if flags.get("stop_after") == "B2":
            return nc

        with contextlib.ExitStack() as st:
            def sb(name, shape, dt):
                return st.enter_context(nc.sbuf_tensor(name, list(shape), dt))

            def ps(name, shape, dt):
                return st.enter_context(nc.psum_tensor(name, list(shape), dt))

            def rot(name, shape, dt, n):
                return Rot([sb("%s%d" % (name, i), shape, dt) for i in range(n)], name)

            slotbase[0] = 0.0
            wg = sb("wgD", [128, 8, 2048], BF16)
            wao = sb("waoD", [128, 8, 1024], BF16)
            wro = sb("wroD", [128, 8, 1024], BF16)
            wo = sb("woD", [128, 8, 1024], BF16)
            g1b = [sb("g1bD%d" % j, [128, D], F32) for j in range(2)]
            G2b = [sb("G2bD%d" % j, [128, D], F32) for j in range(2)]
            sh2b = [sb("sh2bD%d" % j, [128, D], F32) for j in range(2)]
            hT_r = rot("hTD", [128, 1024], BF16, 3)
            ya_r = rot("yaD", [128, 1024], BF16, 4)
            yr_r = rot("yrD", [128, 1024], BF16, 4)
            x_r = rot("xD", [128, D], F32, 3)
            gat_r = rot("gatD", [128, D], BF16, 4)
            Y_r = rot("YD", [128, D], F32, 2)
            Tt_r = rot("TtD", [128, D], F32, 2)
            To_r = rot("ToD", [128, D], F32, 1)
            yb_r = rot("ybD", [128, D], BF16, 2)
            yT_r = rot("yTD", [128, 8, 128], BF16, 2)
            xn_r = rot("xnD", [128, D], F32, 2)
            junk = sb("junkD", [128, D], BF16)
            ss2_r = rot("ss2D", [128, 1], F32, 2)
            tmpf_r = rot("tmpfD", [128, D], F32, 2)
            h2b_r = rot("h2bD", [128, D], BF16, 2)
            h2T_r = rot("h2TD", [128, 1024], BF16, 2)
            pP_r = Rot([ps("pPD%d" % i, [128, 512], F32) for i in range(6)], "pPD")
            pT1 = ps("pT1D", [128, 8, 128], BF16)
            pT2 = ps("pT2D", [128, 8, 128], BF16)
            for i in range(4):
                DG(lambda e, i=i: e.dma_start(out=wg[:, :, i * 512:(i + 1) * 512], in_=w_in[:, :, 4608 + i * 512:4608 + (i + 1) * 512]), w=["wg%d" % i])
            for (wt, src, nm) in ((wao, w_ao, "wao"), (wro, w_ro, "wro"), (wo, w_o, "wo")):
                for i in range(2):
                    DG(lambda e, i=i, wt=wt, src=src: e.dma_start(out=wt[:, :, i * 512:(i + 1) * 512], in_=src[:, :, i * 512:(i + 1) * 512]), w=["%s%d" % (nm, i)])
            for j in range(2):
                DS(lambda e, j=j: e.dma_start(out=g1b[j][:], in_=mod_scr[j, 2]), w=["g1b"])
                DS(lambda e, j=j: e.dma_start(out=G2b[j][:], in_=mod_scr[j, 4]), w=["G2b"])
                DS(lambda e, j=j: e.dma_start(out=sh2b[j][:], in_=mod_scr[j, 3]), w=["sh2b"])

            def _grp(g):
                cond = g["cond"]
                fl = g["full"]
                base = slotbase[0]
                slotbase[0] += len(fl) + 10
                for oi, t in enumerate(fl):
                    S.slot = base + oi - 2
                    si = g["sbase"] + oi
                    hT, hTk = hT_r.next()
                    DS(lambda e, si=si, hT=hT: e.dma_start(out=hT[:], in_=hT_scr[si]), w=[hTk])
                    ya, yak = ya_r.next()
                    DS(lambda e, si=si, ya=ya: e.dma_start(out=ya[:], in_=ya_scr[si]), w=[yak])
                    yr, yrk = yr_r.next()
                    DS(lambda e, si=si, yr=yr: e.dma_start(out=yr[:], in_=yr_scr[si]), w=[yrk])
                    S.slot = base + oi + 1
                    xt, xk = x_r.next()
                    DS(lambda e, t=t, xt=xt: e.dma_start(out=xt[:], in_=g["x"][t * 128:(t + 1) * 128, :]), w=[xk])
                    slot = si
                    S.slot = base + oi
                    Y, Yk = Y_r.next()
                    Tt, Ttk = Tt_r.next()
                    yb, ybk = yb_r.next()
                    gats = []
                    for br in range(2):
                        gat, gatk = gat_r.next()
                        gats.append((gat, gatk))
                        for hf in range(2):
                            pp, ppk = pP_r.next()
                            for k in range(8):
                                T(lambda e, k=k, pp=pp, hT=hT, br=br, hf=hf: e.matmul(pp[:], lhsT=hT[:, k * 128:(k + 1) * 128],
                                                                                      rhs=wg[:, k, br * 1024 + hf * 512:br * 1024 + (hf + 1) * 512], start=(k == 0), stop=(k == 7)),
                                  r=[hTk, "wg%d" % (2 * br + hf)], w=[ppk])
                            A(lambda e, pp=pp, hf=hf, gat=gat: e.activation(out=gat[:, hf * 512:(hf + 1) * 512], in_=pp[:], func=AF.Sigmoid), r=[ppk], w=[gatk + "_%d" % hf])
                    S.slot = base + oi + 1
                    for br in range(2):
                        src, srck = (ya, yak) if br == 0 else (yr, yrk)
                        wt, wn = (wao, "wao") if br == 0 else (wro, "wro")
                        gat, gatk = gats[br]
                        for hf in range(2):
                            pp, ppk = pP_r.next()
                            for k in range(8):
                                T(lambda e, k=k, pp=pp, src=src, wt=wt, hf=hf: e.matmul(pp[:], lhsT=src[:, k * 128:(k + 1) * 128], rhs=wt[:, k, hf * 512:(hf + 1) * 512],
                                                                                        start=(k == 0), stop=(k == 7)), r=[srck, "%s%d" % (wn, hf)], w=[ppk])
                            if br == 0:
                                V(lambda e, pp=pp, hf=hf, gat=gat, Y=Y: e.tensor_tensor(out=Y[:, hf * 512:(hf + 1) * 512], in0=pp[:], in1=gat[:, hf * 512:(hf + 1) * 512], op=ALU.mult),
                                  r=[ppk, gatk + "_%d" % hf], w=[Yk + "_%d" % hf])
                            else:
                                V(lambda e, pp=pp, hf=hf, gat=gat, Tt=Tt: e.tensor_tensor(out=Tt[:, hf * 512:(hf + 1) * 512], in0=pp[:], in1=gat[:, hf * 512:(hf + 1) * 512], op=ALU.mult),
                                  r=[ppk, gatk + "_%d" % hf], w=[Ttk + "_%d" % hf])
                                G(lambda e, hf=hf, Y=Y, Tt=Tt, yb=yb: e.tensor_tensor(out=yb[:, hf * 512:(hf + 1) * 512], in0=Y[:, hf * 512:(hf + 1) * 512], in1=Tt[:, hf * 512:(hf + 1) * 512], op=ALU.add),
                                  r=[Yk + "_%d" % hf, Ttk + "_%d" % hf], w=[ybk + "_%d" % hf])
                    S.slot = base + oi + 2
                    for k in range(8):
                        T(lambda e, k=k, yb=yb: e.transpose(out=pT1[:, k, :], in_=yb[:, k * 128:(k + 1) * 128], identity=ident[:]), r=[ybk + "_%d" % (k // 4)], w=["pT1D"])
                    yT, yTk = yT_r.next()
                    A(lambda e, yT=yT: e.copy(out=yT[:], in_=pT1[:]), r=["pT1D"], w=[yTk])
                    S.slot = base + oi + 3
                    xn, xnk = xn_r.next()
                    To, Tok = To_r.next()
                    for hf in range(2):
                        pp, ppk = pP_r.next()
                        for k in range(8):
                            T(lambda e, k=k, pp=pp, hf=hf, yT=yT: e.matmul(pp[:], lhsT=yT[:, k, :], rhs=wo[:, k, hf * 512:(hf + 1) * 512], start=(k == 0), stop=(k == 7)),
                              r=[yTk, "wo%d" % hf], w=[ppk])
                        V(lambda e, pp=pp, hf=hf, cond=cond, To=To: e.tensor_tensor(out=To[:, hf * 512:(hf + 1) * 512], in0=pp[:], in1=g1b[cond][:, hf * 512:(hf + 1) * 512], op=ALU.mult),
                          r=[ppk, "g1b"], w=[Tok + "_%d" % hf])
                        G(lambda e, hf=hf, xn=xn, xt=xt, To=To: e.tensor_tensor(out=xn[:, hf * 512:(hf + 1) * 512], in0=To[:, hf * 512:(hf + 1) * 512], in1=xt[:, hf * 512:(hf + 1) * 512], op=ALU.add),
                          r=[Tok + "_%d" % hf, xk], w=[xnk + "_%d" % hf])
                    DS(lambda e, xn=xn, slot=slot: e.dma_start(out=xs_scr[slot], in_=xn[:]), r=[xnk + "_0", xnk + "_1"], w=["xs_scr"], sk=xnk)
                    S.slot = base + oi + 4
                    ss2, ss2k = ss2_r.next()
                    tmpf, tmpfk = tmpf_r.next()
                    h2b, h2bk = h2b_r.next()
                    A(lambda e, xn=xn, ss2=ss2: e.activation(out=junk[:], in_=xn[:], func=AF.Square, scale=float(D ** -0.5), accum_out=ss2[:]), r=[xnk + "_0", xnk + "_1"], w=[ss2k])
                    A(lambda e, ss2=ss2: e.activation(out=ss2[:], in_=ss2[:], func=AF.Sqrt, bias=EPS), r=[ss2k], w=[ss2k])
                    V(lambda e, ss2=ss2: e.reciprocal(out=ss2[:], in_=ss2[:]), r=[ss2k], w=[ss2k])
                    V(lambda e, xn=xn, cond=cond, ss2=ss2, tmpf=tmpf: e.scalar_tensor_tensor(out=tmpf[:], in0=xn[:], scalar=ss2[:, 0:1], in1=G2b[cond][:], op0=ALU.mult, op1=ALU.mult),
                      r=[xnk + "_0", xnk + "_1", ss2k, "G2b"], w=[tmpfk])
                    G(lambda e, cond=cond, tmpf=tmpf, h2b=h2b: e.tensor_tensor(out=h2b[:], in0=tmpf[:], in1=sh2b[cond][:], op=ALU.add), r=[tmpfk, "sh2b"], w=[h2bk])
                    S.slot = base + oi + 5
                    for k in range(8):
                        T(lambda e, k=k, h2b=h2b: e.transpose(out=pT2[:, k, :], in_=h2b[:, k * 128:(k + 1) * 128], identity=ident[:]), r=[h2bk], w=["pT2D"])
                    h2T, h2Tk = h2T_r.next()
                    A(lambda e, h2T=h2T: e.copy(out=h2T[:], in_=pT2[:].rearrange("p k d -> p (k d)")), r=["pT2D"], w=[h2Tk])
                    DS(lambda e, h2T=h2T, slot=slot: e.dma_start(out=h2_scr[slot], in_=h2T[:]), r=[h2Tk], w=["h2_scr"])
            for g_ in groups:
                _grp(g_)
            S.emit_phase()

        if flags.get("stop_after") == "B3":
            return nc
        with contextlib.ExitStack() as st:
            def sb(name, shape, dt):
                return st.enter_context(nc.sbuf_tensor(name, list(shape), dt))

            def ps(name, shape, dt):
                return st.enter_context(nc.psum_tensor(name, list(shape), dt))

            wu = sb("wuE", [128, 8, 5632], BF16)
            slotbase[0] = 0.0
            wd = sb("wdE", [128, 22, 1024], BF16)
            g2b = [sb("g2bE%d" % j, [128, D], F32) for j in range(2)]
            fgb = sb("fgbE", [128, D], F32)
            cvt_ = [sb("cvE%d" % j, [128, 4, 44], F32) for j in range(2)]
            h2_r = Rot([sb("h2E%d" % i, [128, 8, 386], BF16) for i in range(2)], "h2E")
            act_r = Rot([sb("actE%d" % i, [128, 22, 384], BF16) for i in range(1)], "actE")
            t1_r = Rot([sb("t1E%d" % i, [128, 384], F32) for i in range(6)], "t1E")
            sgl_r = Rot([sb("sglE%d" % i, [128, 384], F32) for i in range(2)], "sglE")
            xn_r = Rot([sb("xnE%d" % i, [128, D], F32) for i in range(1)], "xnE")
            xf_r = Rot([sb("xfE%d" % i, [128, D], F32) for i in range(1)], "xfE")
            junk = sb("junkE", [128, D], F32)
            ssf = sb("ssfE", [128, 1], F32)
            yo_r = Rot([sb("yoE%d" % i, [128, D], F32) for i in range(1)], "yoE")
            pU_r = Rot([ps("pUE%d" % i, [128, 512], F32) for i in range(6)], "pUE")
            pD_r = Rot([ps("pDE%d" % i, [128, 512], F32) for i in range(2)], "pDE")
            for i in (0, 5, 1, 6, 2, 7, 3, 8, 4, 9, 10):
                DG(lambda e, i=i: e.dma_start(out=wu[:, :, i * 512:(i + 1) * 512], in_=w_up[:, :, i * 512:(i + 1) * 512]), w=["wu%d" % i])
            for i in range(11):
                DG(lambda e, i=i: e.dma_start(out=wd[:, 2 * i:2 * i + 2, :], in_=w_dn[:, 2 * i:2 * i + 2, :]), w=["wd%d" % i])
            for j in range(2):
                DS(lambda e, j=j: e.dma_start(out=g2b[j][:], in_=mod_scr[j, 5]), w=["g2b"])
            DS(lambda e: e.dma_start(out=fgb[:], in_=fg.partition_broadcast(128)), w=["fgb"])
            DS(lambda e: e.dma_start(out=cvt_[0][:], in_=convs), w=["cv"])
            DS(lambda e: e.dma_start(out=cvt_[1][:], in_=convp), w=["cv"])
            def _grp(g):
                cond = g["cond"]
                cvw = cvt_[g["tb"]]
                own = g["own"]
                mblocks = [list(range(i, min(i + 3, own))) for i in range(0, own, 3)]
                def _blk(blk):
                    n = 128 * len(blk)
                    bb = slotbase[0]
                    slotbase[0] += 40
                    S.slot = bb - 20
                    h2, h2k = h2_r.next()
                    wkeys = []
                    t0 = blk[0]
                    if t0 == 0:
                        G(lambda e, h2=h2: e.memset(h2[:, :, 0:1], 0.0), w=[h2k + "_L"])
                    else:
                        DS(lambda e, h2=h2, t0=t0: e.dma_start(out=h2[:, :, 0:1], in_=h2_scr[g["sbase"] + t0 - 1].rearrange("p (k d) -> p k d", k=8)[:, :, 127:128], allow_slow_non_contiguous=True), w=[h2k + "_L"])
                    for j, t in enumerate(blk):
                        DS(lambda e, h2=h2, j=j, t=t: e.dma_start(out=h2[:, :, 1 + j * 128:1 + (j + 1) * 128], in_=h2_scr[g["sbase"] + t].rearrange("p (k d) -> p k d", k=8)),
                           w=[h2k + "_%d" % j])
                    t1 = blk[-1] + 1
                    if t1 < len(g["full"]):
                        DS(lambda e, h2=h2, t1=t1, n=n: e.dma_start(out=h2[:, :, n + 1:n + 2], in_=h2_scr[g["sbase"] + t1].rearrange("p (k d) -> p k d", k=8)[:, :, 0:1], allow_slow_non_contiguous=True), w=[h2k + "_R"])
                    else:
                        G(lambda e, h2=h2, n=n: e.memset(h2[:, :, n + 1:n + 2], 0.0), w=[h2k + "_R"])
                    h2keys = [h2k + "_L", h2k + "_R"] + [h2k + "_%d" % j for j in range(len(blk))]
                    act, actk = act_r.next()
                    for c in range(22):
                        res = []
                        for which in range(2):
                            ch = c + 22 * which
                            S.slot = bb + c
                            pu, puk = pU_r.next()
                            for k in range(8):
                                T(lambda e, k=k, pu=pu, ch=ch, h2=h2, n=n: e.matmul(pu[:, 0:n + 2], lhsT=wu[:, k, ch * 128:(ch + 1) * 128], rhs=h2[:, k, 0:n + 2],
                                                                                    start=(k == 0), stop=(k == 7)), r=h2keys + ["wu%d" % (ch // 4)], w=[puk])
                            S.slot = bb + c + 1
                            t1b, t1k = t1_r.next()
                            A(lambda e, pu=pu, ch=ch, t1b=t1b, n=n, cvw=cvw: e.activation(out=t1b[:, 0:n], in_=pu[:, 1:n + 1], func=AF.Identity, scale=cvw[:, 1, ch:ch + 1], bias=cvw[:, 3, ch:ch + 1]),
                              r=[puk, "cv"], w=[t1k])
                            S.slot = bb + c + 1.5
                            V(lambda e, pu=pu, ch=ch, t1b=t1b, n=n, cvw=cvw: e.scalar_tensor_tensor(out=t1b[:, 0:n], in0=pu[:, 0:n], scalar=cvw[:, 0, ch:ch + 1], in1=t1b[:, 0:n],
                                                                                                    op0=ALU.mult, op1=ALU.add), r=[puk, t1k, "cv"], w=[t1k])
                            S.slot = bb + c + 2
                            V(lambda e, pu=pu, ch=ch, t1b=t1b, n=n, cvw=cvw: e.scalar_tensor_tensor(out=t1b[:, 0:n], in0=pu[:, 2:n + 2], scalar=cvw[:, 2, ch:ch + 1], in1=t1b[:, 0:n],
                                                                                                    op0=ALU.mult, op1=ALU.add), r=[puk, t1k, "cv"], w=[t1k])
                            res.append((t1b, t1k))
                        (ta, tak), (tg, tgk) = res
                        S.slot = bb + c + 3
                        sgl, sglk = sgl_r.next()
                        A(lambda e, tg=tg, sgl=sgl, n=n: e.activation(out=sgl[:, 0:n], in_=tg[:, 0:n], func=AF.Silu), r=[tgk], w=[sglk])
                        G(lambda e, ta=ta, sgl=sgl, act=act, c=c, n=n: e.tensor_tensor(out=act[:, c, 0:n], in0=ta[:, 0:n], in1=sgl[:, 0:n], op=ALU.mult), r=[tak, sglk], w=[actk + "_%d" % c])
                    actkeys = [actk + "_%d" % c for c in range(22)]
                    S.slot = bb + 26
                    for j, t in enumerate(blk):
                        slot = g["sbase"] + t
                        xn, xnk = xn_r.next()
                        DS(lambda e, xn=xn, slot=slot: e.dma_start(out=xn[:], in_=xs_scr[slot]), w=[xnk])
                        xf, xfk = xf_r.next()
                        for hf in range(2):
                            pd, pdk = pD_r.next()
                            for c in range(22):
                                T(lambda e, c=c, pd=pd, act=act, j=j, hf=hf: e.matmul(pd[:], lhsT=act[:, c, j * 128:(j + 1) * 128], rhs=wd[:, c, hf * 512:(hf + 1) * 512],
                                                                                      start=(c == 0), stop=(c == 21)), r=actkeys + ["wd%d" % (c // 2)], w=[pdk])
                            V(lambda e, pd=pd, hf=hf, xf=xf, cond=cond: e.tensor_tensor(out=xf[:, hf * 512:(hf + 1) * 512], in0=pd[:], in1=g2b[cond][:, hf * 512:(hf + 1) * 512], op=ALU.mult),
                              r=[pdk, "g2b"], w=[xfk + "_%d" % hf])
                            G(lambda e, hf=hf, xf=xf, xn=xn: e.tensor_tensor(out=xf[:, hf * 512:(hf + 1) * 512], in0=xf[:, hf * 512:(hf + 1) * 512], in1=xn[:, hf * 512:(hf + 1) * 512], op=ALU.add),
                              r=[xfk + "_%d" % hf, xnk], w=[xfk + "_%d" % hf])
                        xfkeys = [xfk + "_0", xfk + "_1"]
                        A(lambda e, xf=xf: e.activation(out=junk[:], in_=xf[:], func=AF.Square, accum_out=ssf[:]), r=xfkeys, w=["junkE", "ssfE"])
                        V(lambda e: e.tensor_scalar(out=ssf[:], in0=ssf[:], scalar1=1.0 / D, scalar2=EPS, op0=ALU.mult, op1=ALU.add), r=["ssfE"], w=["ssfE"])
                        A(lambda e: e.activation(out=ssf[:], in_=ssf[:], func=AF.Sqrt), r=["ssfE"], w=["ssfE"])
                        V(lambda e: e.reciprocal(out=ssf[:], in_=ssf[:]), r=["ssfE"], w=["ssfE"])
                        yo, yok = yo_r.next()
                        V(lambda e, xf=xf, yo=yo: e.scalar_tensor_tensor(out=yo[:], in0=xf[:], scalar=ssf[:, 0:1], in1=fgb[:], op0=ALU.mult, op1=ALU.mult),
                          r=xfkeys + ["ssfE", "fgb"], w=[yok])
                        if g["samp"]:
                            DS(lambda e, yo=yo, t=t: e.dma_start(out=ys[t * 128:(t + 1) * 128, :], in_=yo[:]), r=[yok], w=["ys"])
                        else:
                            DS(lambda e, yo=yo, t=t, g=g: e.dma_start(out=yp[g["p"] * 256 + t * 128:g["p"] * 256 + (t + 1) * 128, :], in_=yo[:]), r=[yok], w=["yp"])
                for blk_ in mblocks:
                    _blk(blk_)
            for g_ in groups:
                _grp(g_)
            S.emit_phase()
    return nc


def rope_att_ops(V, G, src, srck, tab, tabk, out, outk, H, tA, tB, tAk, tBk):
    tAv = tA[:].rearrange("p a b -> p (a b)")[:, 0:H * 128].rearrange("p (h d) -> p h d", h=H)
    tBv = tB[:].rearrange("p a b -> p (a b)")[:, 0:H * 128].rearrange("p (h d) -> p h d", h=H)
    V(lambda e: e.tensor_tensor(out=tAv, in0=src[:], in1=tab[:, 0, :].unsqueeze(1).to_broadcast([128, H, 128]), op=ALU.mult), r=[srck, tabk], w=[tAk])
    s5 = src[:].rearrange("p h (a b c) -> p h a b c", a=2, b=2)
    o5 = tBv.rearrange("p h (a b c) -> p h a b c", a=2, b=2)
    sn = tab[:, 1, :].rearrange("p (a b c) -> p a b c", a=2, b=2)
    for b in range(2):
        V(lambda e, b=b: e.tensor_tensor(out=o5[:, :, :, b, :], in0=s5[:, :, :, 1 - b, :], in1=sn[:, :, b, :].unsqueeze(1).to_broadcast([128, H, 2, 32]), op=ALU.mult),
          r=[srck, tabk], w=[tBk])
    G(lambda e: e.tensor_tensor(out=out[:], in0=tAv, in1=tBv, op=ALU.add), r=[tAk, tBk], w=[outk])


def rope_ret_ops(V, G, src3, srck, tab, tabk, out, outk, tA, tB, tAk, tBk):
    V(lambda e: e.tensor_tensor(out=tA[:], in0=src3, in1=tab[:, 0, :].unsqueeze(1).to_broadcast([128, 8, 64]), op=ALU.mult), r=[srck, tabk], w=[tAk])
    s4 = src3.rearrange("p h (b c) -> p h b c", b=2)
    o4 = tB[:].rearrange("p h (b c) -> p h b c", b=2)
    sn = tab[:, 1, :].rearrange("p (b c) -> p b c", b=2)
    for b in range(2):
        V(lambda e, b=b: e.tensor_tensor(out=o4[:, :, b, :], in0=s4[:, :, 1 - b, :], in1=sn[:, b, :].unsqueeze(1).to_broadcast([128, 8, 32]), op=ALU.mult),
          r=[srck, tabk], w=[tBk])
    G(lambda e: e.tensor_tensor(out=out[:], in0=tA[:], in1=tB[:], op=ALU.add), r=[tAk, tBk], w=[outk])


def _kc(w, kc):
    n = w.shape[1]
    return np.ascontiguousarray(w.reshape(kc, 128, n).transpose(1, 0, 2))


def _consts():
    inv = (10000.0 ** (-np.arange(32, dtype=np.float32) / np.float32(32))).astype(np.float32)
    j = np.arange(128)[:, None]
    i = np.arange(128)[None, :]
    tabs = np.stack([np.maximum(i - j, 0), (i >= j), np.maximum(j - i, 0), (j >= i)], axis=1).astype(np.float32)
    p = np.arange(128, dtype=np.float32)
    tabe = np.stack([p + 1, 128 - p, 127 - p, p], axis=1).astype(np.float32)
    return inv, np.ascontiguousarray(tabs), np.ascontiguousarray(tabe)


def _rope_tab(pos, inv):
    ang = pos.astype(np.float32)[:, None] * inv[None, :]
    c, s = np.cos(ang).astype(np.float32), np.sin(ang).astype(np.float32)
    return np.concatenate([c, c], 1), np.concatenate([-s, s], 1)


def prep_inputs(inp):
    f = lambda k: np.asarray(inp[k], dtype=np.float32)
    inv, tabs, tabe = _consts()
    shared = {
        "w_in": _kc(f("w_in")[0], 8), "w_mod": _kc(f("w_mod")[0], 8), "b_mod": f("b_mod").reshape(1, 6144),
        "w_ao": _kc(f("w_att_o")[0], 8), "w_ro": _kc(f("w_ret_o")[0], 8), "w_o": _kc(f("w_out")[0], 8),
        "w_up": _kc(f("w_up")[0], 8), "w_dn": _kc(f("w_down")[0], 22),
        "n1g": f("norm1_g").reshape(1, D), "n2g": f("norm2_g").reshape(1, D), "fg": f("final_g").reshape(1, D),
        "qg": f("q_norm_g").reshape(1, 128), "kg": f("k_norm_g").reshape(1, 128),
        "tabs": tabs, "tabe": tabe,
    }
    cw, cb = f("conv_w")[0], f("conv_b")[0]

    def convtab(rev):
        taps = [cw[2 - k] if rev else cw[k] for k in range(3)] + [cb]
        a = np.stack(taps, 0).reshape(4, 44, 128).transpose(2, 0, 1)
        return np.ascontiguousarray(a)

    cr, sr = _rope_tab(np.arange(256), inv)
    shared["rope_rp"] = np.ascontiguousarray(np.stack([cr, sr], 1))
    shared["convp"] = convtab(False)
    xsamp, xprm = f("x_sample"), f("x_prompt")
    cc, cctx = f("c"), f("c_ctx")
    ck, cv = f("cache_k"), f("cache_v")
    sf, sbw = f("state_ret_fwd"), f("state_ret_bwd")
    df, db = f("decay_fwd")[0], f("decay_bwd")[0]
    maps = []
    for c in range(8):
        b, rev = c // 2, c % 2
        t = np.arange(4096)
        if rev:
            t = t[::-1]
        m = dict(shared)
        m["xs"] = np.ascontiguousarray(xsamp[b][t])
        m["xp"] = np.ascontiguousarray(xprm[2 * c:2 * c + 2].reshape(512, D))
        m["cT"] = np.ascontiguousarray(np.stack([cc[b], cctx], 0).reshape(2, 8, 128).transpose(2, 1, 0))
        m["ck"] = np.ascontiguousarray(ck[b, 0])
        m["cv"] = np.ascontiguousarray(cv[b, 0])
        a0, b0 = (sbw, sf) if rev else (sf, sbw)
        m["s0f"] = np.ascontiguousarray(a0[b, 0].transpose(1, 0, 2))
        m["s0b"] = np.ascontiguousarray(b0[b, 0].transpose(1, 0, 2))
        m["dec"] = np.concatenate([db, df, df, db] if rev else [df, db, df, db]).reshape(1, 32).astype(np.float32)
        m["convs"] = convtab(bool(rev))
        c1, s1 = _rope_tab(t // 64, inv)
        c2, s2 = _rope_tab(t % 64, inv)
        m["rope_att"] = np.ascontiguousarray(np.stack([np.concatenate([c1, c2], 1), np.concatenate([s1, s2], 1)], 1))
        c3, s3 = _rope_tab(512 + t, inv)
        m["rope_rs"] = np.ascontiguousarray(np.stack([c3, s3], 1))
        maps.append(m)
    return maps


_NC_CACHE = {}


def run_device(inp, flags=None):
    flags = flags or {}
    key = tuple(sorted(flags.items()))
    if key not in _NC_CACHE:
        _NC_CACHE[key] = build_nc(flags)
    nc = _NC_CACHE[key]
    maps = prep_inputs(inp)
    res = run_bass_kernel_spmd(nc, maps, core_ids=list(range(8)))
    return res.results


def assemble(results):
    y_prompt = np.zeros((16, 256, D), np.float32)
    y_sample = np.zeros((4, 4096, D), np.float32)
    nck = np.zeros((16, 1, 2, 256, 128), np.float32)
    ncv = np.zeros((16, 1, 2, 256, 128), np.float32)
    nf = np.zeros((16, 1, 8, 64, 128), np.float32)
    nb = np.zeros((16, 1, 8, 64, 128), np.float32)
    for c in range(8):
        r = results[c]
        b, rev = c // 2, c % 2
        if rev:
            y_sample[b, 2048:4096] = r["ys"][::-1]
        else:
            y_sample[b, 0:2048] = r["ys"]
        y_prompt[2 * c:2 * c + 2] = r["yp"].reshape(2, 256, D)
        nck[2 * c:2 * c + 2, 0] = r["nk"]
        ncv[2 * c:2 * c + 2, 0] = r["nv"]
        nf[2 * c:2 * c + 2, 0] = r["nsf"].transpose(0, 2, 1, 3)
        nb[2 * c:2 * c + 2, 0] = r["nsb"].transpose(0, 2, 1, 3)
    return (y_prompt, y_sample, nck, ncv, nf, nb)


def kernel(**inputs):
    return assemble(run_device(inputs))
```
